# Optimizing a Trainium2 kernel written in Bass

```python
import math
import jax, jax.numpy as jnp
from jax import lax
import numpy as np

D_MODEL = 1024
BATCH = 4
SEQ = 4096
DEPTH = 2

N_META = 16
CHUNK = 128
PAD = (-N_META) % CHUNK
RET_HEADS = 4
RET_DK = 48
RET_DV = 96
RET_QK = RET_HEADS * RET_DK
RET_W = RET_HEADS * RET_DV
POOL_WINDOWS = (2, 4, 8, 16)
POOL_GROUP = 64
POOL_W = len(POOL_WINDOWS) * POOL_GROUP
FOX_HEADS = 6
FOX_DH = 64
FOX_W = FOX_HEADS * FOX_DH
D_MIX = RET_W + POOL_W + FOX_W
IN_SPLITS = (RET_QK, RET_QK, RET_W, RET_W, POOL_W, FOX_W, FOX_W, FOX_W, FOX_HEADS)
D_IN = sum(IN_SPLITS)
D_FF = -(-8 * D_MODEL // (3 * 256)) * 256
ROPE_BASE = 10000.0
LN_EPS = 1e-5
NEG_INF = -1e30
ALPHA = (2.0 * DEPTH) ** 0.25
BETA = (8.0 * DEPTH) ** -0.25

kernel_name = 'hymba_style_retention_pool_fox_deepnorm'


def layer_norm(x, g, b):
    xf = x.astype(jnp.float32)
    mu = jnp.mean(xf, axis=-1, keepdims=True)
    var = jnp.mean(jnp.square(xf - mu), axis=-1, keepdims=True)
    y = (xf - mu) * lax.rsqrt(var + LN_EPS)
    return (y * g + b).astype(x.dtype)


def pad_front(a):
    widths = [(0, 0), (PAD, 0)] + [(0, 0)] * (a.ndim - 2)
    return jnp.pad(a, widths)


def rotary(x, cos, sin):
    half = x.shape[-1] // 2
    x1, x2 = x[..., :half], x[..., half:]
    return jnp.concatenate([x1 * cos - x2 * sin, x1 * sin + x2 * cos], axis=-1)


def retention_chunkwise(q, k, v):
    Bsz, Lp, H, dk = q.shape
    dv = v.shape[-1]
    n = Lp // CHUNK
    gamma = 1.0 - 2.0 ** (-5.0 - jnp.arange(H, dtype=jnp.float32))
    lg = jnp.log(gamma)
    i = jnp.arange(CHUNK, dtype=jnp.float32)
    diff = i[:, None] - i[None, :]
    dmask = jnp.where(diff >= 0, jnp.exp(lg[:, None, None] * jnp.maximum(diff, 0.0)), 0.0)
    xi = jnp.exp(lg[:, None] * (i + 1.0))
    zeta = jnp.exp(lg[:, None] * (CHUNK - 1.0 - i))
    chunk_decay = jnp.exp(lg * CHUNK)
    qc = q.astype(jnp.float32).reshape(Bsz, n, CHUNK, H, dk)
    kc = k.astype(jnp.float32).reshape(Bsz, n, CHUNK, H, dk) * (dk ** -0.5)
    vc = v.astype(jnp.float32).reshape(Bsz, n, CHUNK, H, dv)
    scores = jnp.einsum('bnihd,bnjhd->bnhij', qc, kc) * dmask
    inner = jnp.einsum('bnhij,bnjhe->bnihe', scores, vc)
    kv = jnp.einsum('bnjhd,hj,bnjhe->nbhde', kc, zeta, vc)

    def step(state, kv_n):
        return state * chunk_decay[None, :, None, None] + kv_n, state

    _, prev = lax.scan(step, jnp.zeros_like(kv[0]), kv)
    cross = jnp.einsum('bnihd,nbhde,hi->bnihe', qc, prev, xi)
    return (inner + cross).reshape(Bsz, Lp, H, dv)


def multiscale_pool(u, pool_w, pool_scale):
    Bsz, L, _ = u.shape
    G = len(POOL_WINDOWS)
    uf = u.astype(jnp.float32).reshape(Bsz, L, G, POOL_GROUP)
    cs = jnp.concatenate([jnp.zeros((Bsz, 1, G, POOL_GROUP), jnp.float32),
                          jnp.cumsum(uf, axis=1)], axis=1)
    t = jnp.arange(L)
    outs = []
    for g, w in enumerate(POOL_WINDOWS):
        lo = jnp.maximum(t + 1 - w, 0)
        cnt = (t + 1 - lo).astype(jnp.float32)
        window_sum = cs[:, t + 1, g] - cs[:, lo, g]
        outs.append(window_sum / cnt[None, :, None] - uf[:, :, g])
    pooled = jnp.stack(outs, axis=2)
    y = jnp.einsum('blgc,gcd->blgd', pooled, pool_w.astype(jnp.float32)).reshape(Bsz, L, POOL_W)
    return y * pool_scale


def forgetting_attention(q, k, v, logf):
    Bsz, Lp, H, d = q.shape
    c = jnp.cumsum(logf.astype(jnp.float32), axis=1).transpose(0, 2, 1)
    qf = q.astype(jnp.float32) * (d ** -0.5)
    kf = k.astype(jnp.float32)
    vf = v.astype(jnp.float32)
    kpos = jnp.arange(Lp)
    key_ok = kpos >= PAD
    nb = Lp // CHUNK

    def block(ib):
        start = ib * CHUNK
        qb = lax.dynamic_slice_in_dim(qf, start, CHUNK, axis=1)
        cb = lax.dynamic_slice_in_dim(c, start, CHUNK, axis=2)
        qpos = start + jnp.arange(CHUNK)
        s = jnp.einsum('bihd,bjhd->bhij', qb, kf)
        s = s + cb[..., :, None] - c[..., None, :]
        mask = (kpos[None, :] <= qpos[:, None]) & key_ok[None, :]
        s = jnp.where(mask, s, NEG_INF)
        p = jax.nn.softmax(s, axis=-1)
        return jnp.einsum('bhij,bjhd->bihd', p, vf)

    out = lax.map(block, jnp.arange(nb))
    return out.transpose(1, 0, 2, 3, 4).reshape(Bsz, Lp, H, d)


def head_group_norm(o, g):
    of = o.astype(jnp.float32)
    mu = jnp.mean(of, axis=-1, keepdims=True)
    var = jnp.mean(jnp.square(of - mu), axis=-1, keepdims=True)
    y = ((of - mu) * lax.rsqrt(var + LN_EPS)).reshape(o.shape[0], o.shape[1], -1)
    return y * g


def hybrid_layer(h, w_in, b_f, ret_gn_g, pool_w, pool_scale, w_out,
                 ln1_g, ln1_b, w1, w3, w2, ln2_g, ln2_b, cos, sin):
    Bsz, L, _ = h.shape
    proj = jnp.einsum('bld,de->ble', h, w_in)
    parts = []
    off = 0
    for sz in IN_SPLITS:
        parts.append(proj[..., off:off + sz])
        off += sz
    q_r, k_r, v_r, g_r, u_p, q_f, k_f, v_f, f_logit = parts

    qr = rotary(pad_front(q_r.reshape(Bsz, L, RET_HEADS, RET_DK)), cos, sin)
    kr = rotary(pad_front(k_r.reshape(Bsz, L, RET_HEADS, RET_DK)), cos, sin)
    vr = pad_front(v_r.reshape(Bsz, L, RET_HEADS, RET_DV))
    o_r = retention_chunkwise(qr, kr, vr)[:, PAD:]
    o_r = jax.nn.silu(g_r.astype(jnp.float32)) * head_group_norm(o_r, ret_gn_g)

    o_p = multiscale_pool(u_p, pool_w, pool_scale)

    logf = jax.nn.log_sigmoid(f_logit.astype(jnp.float32) + b_f)
    o_f = forgetting_attention(pad_front(q_f.reshape(Bsz, L, FOX_HEADS, FOX_DH)),
                               pad_front(k_f.reshape(Bsz, L, FOX_HEADS, FOX_DH)),
                               pad_front(v_f.reshape(Bsz, L, FOX_HEADS, FOX_DH)),
                               pad_front(logf))[:, PAD:].reshape(Bsz, L, FOX_W)

    mix = jnp.concatenate([o_r, o_p, o_f], axis=-1)
    mix = jnp.einsum('ble,ed->bld', mix, w_out)
    h = layer_norm(ALPHA * h + mix, ln1_g, ln1_b)

    a = jnp.einsum('bld,df->blf', h, w1)
    b = jnp.einsum('bld,df->blf', h, w3)
    y = jnp.einsum('blf,fd->bld', jax.nn.silu(a) * b, w2)
    return layer_norm(ALPHA * h + y, ln2_g, ln2_b)


def setup_inputs(seed: int = 0) -> dict:
    key = jax.random.key(seed)
    ks = jax.random.split(key, 20)
    f32 = jnp.float32
    x = jax.random.normal(ks[0], (BATCH, SEQ, D_MODEL), f32)
    meta = jax.random.normal(ks[1], (N_META, D_MODEL), f32)
    ln_emb_g = 1.0 + 0.01 * jax.random.normal(ks[2], (D_MODEL,), f32)
    ln_emb_b = 0.01 * jax.random.normal(ks[3], (D_MODEL,), f32)
    w_in = jax.random.normal(ks[4], (DEPTH, D_MODEL, D_IN), f32) * D_MODEL ** -0.5
    b_f = jax.random.uniform(ks[5], (DEPTH, FOX_HEADS), f32, 1.0, 4.0)
    ret_gn_g = 1.0 + 0.01 * jax.random.normal(ks[6], (DEPTH, RET_W), f32)
    pool_w = jax.random.normal(ks[7], (DEPTH, len(POOL_WINDOWS), POOL_GROUP, POOL_GROUP), f32) * POOL_GROUP ** -0.5
    pool_scale = 1.0 + 0.1 * jax.random.normal(ks[8], (DEPTH, POOL_W), f32)
    w_out = jax.random.normal(ks[9], (DEPTH, D_MIX, D_MODEL), f32) * (D_MIX ** -0.5) * BETA
    ln1_g = 1.0 + 0.01 * jax.random.normal(ks[10], (DEPTH, D_MODEL), f32)
    ln1_b = 0.01 * jax.random.normal(ks[11], (DEPTH, D_MODEL), f32)
    w_ffn1 = jax.random.normal(ks[12], (DEPTH, D_MODEL, D_FF), f32) * D_MODEL ** -0.5
    w_ffn3 = jax.random.normal(ks[13], (DEPTH, D_MODEL, D_FF), f32) * D_MODEL ** -0.5
    w_ffn2 = jax.random.normal(ks[14], (DEPTH, D_FF, D_MODEL), f32) * (D_FF ** -0.5) * BETA
    ln2_g = 1.0 + 0.01 * jax.random.normal(ks[15], (DEPTH, D_MODEL), f32)
    ln2_b = 0.01 * jax.random.normal(ks[16], (DEPTH, D_MODEL), f32)
    return {'x': x, 'meta': meta, 'ln_emb_g': ln_emb_g, 'ln_emb_b': ln_emb_b,
            'w_in': w_in, 'b_f': b_f, 'ret_gn_g': ret_gn_g, 'pool_w': pool_w,
            'pool_scale': pool_scale, 'w_out': w_out, 'ln1_g': ln1_g, 'ln1_b': ln1_b,
            'w_ffn1': w_ffn1, 'w_ffn3': w_ffn3, 'w_ffn2': w_ffn2,
            'ln2_g': ln2_g, 'ln2_b': ln2_b}


def reference(x, meta, ln_emb_g, ln_emb_b, w_in, b_f, ret_gn_g, pool_w, pool_scale,
              w_out, ln1_g, ln1_b, w_ffn1, w_ffn3, w_ffn2, ln2_g, ln2_b):
    Bsz = x.shape[0]
    h = jnp.concatenate([jnp.broadcast_to(meta[None].astype(x.dtype), (Bsz, N_META, D_MODEL)), x], axis=1)
    h = layer_norm(h, ln_emb_g, ln_emb_b)
    L = h.shape[1]
    Lp = L + PAD
    pos = jnp.arange(Lp, dtype=jnp.float32) - PAD
    inv_freq = ROPE_BASE ** (-jnp.arange(RET_DK // 2, dtype=jnp.float32) / (RET_DK // 2))
    ang = pos[:, None] * inv_freq[None, :]
    cos = jnp.cos(ang)[:, None, :]
    sin = jnp.sin(ang)[:, None, :]
    for l in range(DEPTH):
        h = hybrid_layer(h, w_in[l], b_f[l], ret_gn_g[l], pool_w[l], pool_scale[l], w_out[l],
                         ln1_g[l], ln1_b[l], w_ffn1[l], w_ffn3[l], w_ffn2[l], ln2_g[l], ln2_b[l],
                         cos, sin)
    return h[:, N_META:]
```

```python
import numpy as np
import concourse.bass as bass
import concourse.mybir as mybir
from concourse.bass_utils import run_bass_kernel_spmd

F32 = mybir.dt.float32
BF16 = mybir.dt.bfloat16
AF = mybir.ActivationFunctionType
ALU = mybir.AluOpType


class _Rec:
    def __init__(self):
        self.call = None

    def __getattr__(self, name):
        def f(*a, **k):
            self.call = (name, a, k)
            return self
        return f


def _bind(fn):
    rec = _Rec()
    fn(rec)
    name, a, k = rec.call
    return lambda e: getattr(e, name)(*a, **k)


class Prog:
    ENGS = ('pe', 'act', 'dve', 'pool', 'sp')
    NDMA = 16

    def __init__(self, nc):
        self.nc = nc
        self.ops = {e: [] for e in self.ENGS}
        self.sem = {}
        for e in self.ENGS:
            self.sem['c_' + e] = nc.alloc_semaphore('c_' + e)
        self.dma_sems = {}
        for q in ('sp', 'pool', 'act'):
            self.dma_sems[q] = []
            for i in range(self.NDMA if q == 'sp' else 8):
                nm = 'd%s%d' % (q, i)
                self.sem[nm] = nc.alloc_semaphore(nm)
                self.dma_sems[q].append(nm)
        self.cnt = {s: 0 for s in self.sem}
        self.known = {e: {} for e in self.ENGS}
        self.lastw = {}
        self.readers = {}
        self.pending = {e: [] for e in self.ENGS}
        self.dma_rr = {'sp': 0, 'pool': 0, 'act': 0}
        self.n_ops = 0
        self.stage = ''
        self.tags = {e: [] for e in self.ENGS}

    def sb(self, name, shape, dtype):
        return self.nc.alloc_sbuf_tensor(name, list(shape), dtype)

    def ps(self, name, shape, dtype):
        return self.nc.alloc_psum_tensor(name, list(shape), dtype)

    def _collect(self, reads, writes):
        deps = []
        for r in reads:
            t = self.lastw.get(r)
            if t is not None:
                deps.append(t)
        for w in writes:
            t = self.lastw.get(w)
            if t is not None:
                deps.append(t)
            deps.extend(self.readers.get(w, {}).values())
        return deps

    def _waits(self, eng, deps):
        need = {}
        own = 'c_' + eng
        for sem, val in deps:
            if eng == 'pe' and sem == own:
                continue
            if val is None:
                raise RuntimeError("dependency on unresolved token %s from %s" % (sem, eng))
            if self.known[eng].get(sem, 0) >= val:
                continue
            if need.get(sem, 0) < val:
                need[sem] = val
        for s, v in need.items():
            self.known[eng][s] = v
        return list(need.items())

    def _track(self, tok, reads, writes):
        ws = set(writes)
        for r in reads:
            if r in ws:
                continue
            self.readers.setdefault(r, {})[tok[0]] = tok
        for w in writes:
            self.lastw[w] = tok
            self.readers[w] = {}

    def op(self, eng, fn, reads=(), writes=(), inc=True):
        sem = 'c_' + eng
        tok = [sem, None]
        waits = self._waits(eng, self._collect(reads, writes))
        if inc:
            self.cnt[sem] += 1
            tok[1] = self.cnt[sem]
            for p in self.pending[eng]:
                p[1] = tok[1]
            self.pending[eng] = []
        else:
            self.pending[eng].append(tok)
        self.ops[eng].append((waits, _bind(fn), (sem, 1) if inc else None))
        self.tags[eng].append(self.stage)
        self._track(tok, reads, writes)
        self.n_ops += 1

    def dma(self, eng, out, in_, reads=(), writes=()):
        pool = self.dma_sems[eng]
        i = self.dma_rr[eng]
        self.dma_rr[eng] = (i + 1) % len(pool)
        sem = pool[i]
        deps = self._collect(reads, writes)
        if self.cnt[sem] > 0:
            deps.append([sem, self.cnt[sem]])
        waits = self._waits(eng, deps)
        self.cnt[sem] += 16
        tok = [sem, self.cnt[sem]]
        self.ops[eng].append((waits, lambda e: e.dma_start(out=out, in_=in_), (sem, 16)))
        self.tags[eng].append(self.stage)
        self._track(tok, reads, writes)
        self.n_ops += 1

    def cc(self, kind, groups, in_ap, out_ap, reads=(), writes=()):
        if 'cc' not in self.sem:
            self.sem['cc'] = self.nc.alloc_semaphore('cc')
            self.cnt['cc'] = 0
        waits = self._waits('pool', self._collect(reads, writes))
        self.cnt['cc'] += 1
        tok = ['cc', self.cnt['cc']]
        self.ops['pool'].append((waits, lambda e: e.collective_compute(kind, ALU.bypass, replica_groups=groups,
                                                                        ins=[in_ap], outs=[out_ap]), ('cc', 1)))
        self.tags['pool'].append(self.stage)
        self._track(tok, reads, writes)
        self.n_ops += 1

    def finish(self, out_bufs):
        deps = []
        for b in out_bufs:
            t = self.lastw.get(b)
            if t is not None:
                deps.append(t)
        for pool in self.dma_sems.values():
            for s in pool:
                if self.cnt[s] > 0:
                    deps.append([s, self.cnt[s]])
        if self.cnt.get('cc', 0) > 0:
            deps.append(['cc', self.cnt['cc']])
        waits = self._waits('sp', deps)
        self.ops['sp'].append((waits, None, None))
        self._emit()

    def _replay(self, name, eng):
        for waits, fn, inc in self.ops[name]:
            for s, v in waits:
                eng.wait_ge(self.sem[s], v)
            if fn is None:
                continue
            ins = fn(eng)
            if inc is not None:
                ins.then_inc(self.sem[inc[0]], inc[1])

    def _emit(self):
        with self.nc.Block() as block:
            @block.tensor
            def _(e):
                self._replay('pe', e)

            @block.scalar
            def _(e):
                self._replay('act', e)

            @block.vector
            def _(e):
                self._replay('dve', e)

            @block.gpsimd
            def _(e):
                self._replay('pool', e)

            @block.sync
            def _(e):
                self._replay('sp', e)


D = 1024
D_IN = 2566
D_FF = 2816
NFT = D_FF // 128
PADN = 112
ALPHA = 4.0 ** 0.25
LN_EPS = 1e-5
GAMMAS = [1.0 - 2.0 ** (-5.0 - h) for h in range(4)]
POOL_WINDOWS = (2, 4, 8, 16)
SLOT = 3072
NSLOT = 6


def host_consts(nch, G, role):
    NP = nch + G
    sh = G if role == 1 else 0
    p = np.arange(128)
    ident = np.eye(128, dtype=np.float32)
    triu = (p[:, None] <= p[None, :]).astype(np.float32)
    ones = np.ones((128, 128), np.float32)
    poolA = np.zeros((128, 4, 128), np.float32)
    poolB = np.zeros((128, 4, 128), np.float32)
    poolC = np.zeros((128, 4, 128), np.float32)
    for g, w in enumerate(POOL_WINDOWS):
        for t in range(128):
            for s_ in range(t - w + 1, t + 1):
                if s_ >= 0:
                    poolA[s_, g, t] += 1.0 / w
                else:
                    poolB[128 + s_, g, t] += 1.0 / w
            poolA[t, g, t] -= 1.0
        for tr in range(16):
            lo = max(tr + 1 - w, 0)
            cnt = tr + 1 - lo
            for sr in range(lo, tr + 1):
                poolC[PADN + sr, g, PADN + tr] += 1.0 / cnt
            poolC[PADN + tr, g, PADN + tr] -= 1.0
    poolX = poolC if role == 0 else poolA
    poolY = poolA if role == 0 else poolC
    dec = np.zeros((128, 8), np.float64)
    for h in range(4):
        dec[:, h] = GAMMAS[h] ** (p + 1.0)
        dec[:, 4 + h] = GAMMAS[h] ** (-(p + 1.0)) * 48.0 ** -0.5
    vm = np.ones((128, G + 1), np.float32)
    pn = np.zeros((128, G + 1), np.float32)
    for n in range(G + 1):
        ln_ = n - sh
        if ln_ < 0:
            vm[:, n] = 0.0; pn[:, n] = -1e30
        elif ln_ == 0:
            vm[:, n] = (p >= PADN); pn[:, n] = np.where(p >= PADN, 0.0, -1e30)
    colmask = np.repeat(vm.T.reshape(1, -1), 128, axis=0).astype(np.float32)
    pos = (np.arange(NP * 128, dtype=np.float32) - np.float32(PADN + sh * 128))
    inv_freq = (np.float32(10000.0) ** (-np.arange(24, dtype=np.float32) / np.float32(24))).astype(np.float32)
    ang = (pos[:, None] * inv_freq[None, :]).astype(np.float32)
    cos = np.cos(ang).astype(np.float32).reshape(NP, 128, 24).transpose(1, 0, 2)
    sin = np.sin(ang).astype(np.float32).reshape(NP, 128, 24).transpose(1, 0, 2)
    gt = np.zeros((128, 4, 96), np.float32)
    for h in range(4):
        gt[:, h, :] = GAMMAS[h] ** 128.0
    flags = np.zeros((128, 2), np.float32)
    flags[:, role] = 1.0
    e3 = (p[:, None] == np.arange(3)[None, :]).astype(np.float32)
    parts = [ident, triu, ones, poolA.reshape(128, -1), poolB.reshape(128, -1), poolX.reshape(128, -1),
             poolY.reshape(128, -1), dec.astype(np.float32), vm, pn, flags, e3, gt.reshape(128, -1),
             cos.reshape(128, -1), sin.reshape(128, -1)]
    return (np.ascontiguousarray(np.concatenate(parts, axis=1).astype(np.float32)),
            np.ascontiguousarray(colmask))


def build_program(nch, G=3, pairs=((0, 4), (1, 5), (2, 6), (3, 7)), debug=False):
    depth = 1
    S = (nch - 1) * 128
    NP = nch + G
    nc = bass.Bass("TRN2", target_bir_lowering=False)
    P = Prog(nc)
    din = lambda name, shape: nc.dram_tensor(name, list(shape), F32, kind="ExternalInput").ap()
    xin = din("xin", [NP * 128, D])
    ln_emb_g = din("ln_emb_g", [D]); ln_emb_b = din("ln_emb_b", [D])
    w_in = din("w_in", [depth, D, D_IN]); b_f = din("b_f", [depth, 6])
    ret_gn_g = din("ret_gn_g", [depth, 384]); pool_w = din("pool_w", [depth, 4, 64, 64])
    pool_scale = din("pool_scale", [depth, 256]); w_out = din("w_out", [depth, D, D])
    ln1_g = din("ln1_g", [depth, D]); ln1_b = din("ln1_b", [depth, D])
    w1 = din("w_ffn1", [depth, D, D_FF]); w3 = din("w_ffn3", [depth, D, D_FF]); w2 = din("w_ffn2", [depth, D_FF, D])
    ln2_g = din("ln2_g", [depth, D]); ln2_b = din("ln2_b", [depth, D])
    NCONST = 128 * 3 + 512 * 4 + 8 + 2 * (G + 1) + 2 + 3 + 384 + 48 * NP
    cst_d = din("consts", [128, NCONST])
    cmask_d = din("colmask", [128, (G + 1) * 128])
    out = nc.dram_tensor("out", [S, D], F32, kind="ExternalOutput").ap()
    obuf = nc.dram_tensor("obuf", [G * 128, D], F32, kind="Internal").ap()
    gath = nc.dram_tensor("gath", [2 * G * 128, D], F32, kind="Internal").ap()
    nch_seq = nch
    nch = NP
    wbf = {}
    for l in range(depth):
        wbf[('in', l)] = nc.dram_tensor("wbf_in%d" % l, [D, D_IN], BF16, kind="Internal").ap()
        wbf[('out', l)] = nc.dram_tensor("wbf_out%d" % l, [D, D], BF16, kind="Internal").ap()
        wbf[('w1', l)] = nc.dram_tensor("wbf_w1%d" % l, [D, D_FF], BF16, kind="Internal").ap()
        wbf[('w3', l)] = nc.dram_tensor("wbf_w3%d" % l, [D, D_FF], BF16, kind="Internal").ap()
        wbf[('w2', l)] = nc.dram_tensor("wbf_w2%d" % l, [D_FF, D], BF16, kind="Internal").ap()
    wsrc = {'in': w_in, 'out': w_out, 'w1': w1, 'w3': w3, 'w2': w2}
    RB = 256
    for l in range(depth):
        for nm in ('in', 'out', 'w1', 'w3', 'w2'):
            rows = wbf[(nm, l)].shape[0]
            for r0 in range(0, rows, RB):
                P.dma('pool', wbf[(nm, l)][r0:r0 + RB, :], wsrc[nm][l, r0:r0 + RB, :],
                      writes=['wbf_%s%d_%d' % (nm, l, r0 // RB)])

    def wnames(nm, l, r0, r1):
        return ['wbf_%s%d_%d' % (nm, l, rb) for rb in range(r0 // RB, (r1 - 1) // RB + 1)]

    NT = G * 128
    cst = P.sb("cst", [128, NCONST], F32)
    o = 0
    def cs(n):
        nonlocal o
        v = cst[:, o:o + n]; o += n
        return v
    ident_f = cs(128); triu_f = cs(128); ones_f = cs(128)
    poolA = cs(512).rearrange("p (g t) -> p g t", g=4)
    poolB = cs(512).rearrange("p (g t) -> p g t", g=4)
    poolX = cs(512).rearrange("p (g t) -> p g t", g=4)
    poolY = cs(512).rearrange("p (g t) -> p g t", g=4)
    dec = cs(8); vmaskt = cs(G + 1); padnegt = cs(G + 1); flags = cs(2); e3 = cs(3)
    gtile = cs(384)
    cos = cs(24 * nch).rearrange("p (n f) -> p n f", f=24)
    sin = cs(24 * nch).rearrange("p (n f) -> p n f", f=24)
    P.dma('sp', cst[:], cst_d, writes=['cst'])
    cbf = P.sb("cbf", [128, 384], BF16)
    ident_b = cbf[:, 0:128]; triu_b = cbf[:, 128:256]; ones_b = cbf[:, 256:384]
    P.op('dve', lambda e: e.tensor_copy(cbf[:], cst[:, 0:384]), reads=['cst'], writes=['cbf'])
    cmask_b = P.sb("cmask_b", [128, (G + 1) * 128], BF16)
    P.dma('pool', cmask_b[:], cmask_d, writes=['cmask_b'])

    hres = P.sb("hres", [128, G, D], F32)
    hT = P.sb("hT", [128, 8, NT], BF16)
    KT = P.sb("KT", [128, 3, nch * 128], BF16)
    Vst = P.sb("Vst", [128, nch, 384], BF16)
    qT = P.sb("qT", [128, 3, NT], BF16)
    wring = [P.sb("wr%d" % i, [128, SLOT], BF16) for i in range(NSLOT)]
    fall = P.sb("fall", [128, G, 6], F32)
    rt = [P.sb("rt%d" % i, [128, 8, 24], F32) for i in range(4)]
    rr = P.sb("rr", [128, 8, 48], F32)
    qkr = P.sb("qkr", [128, 8, 48], BF16)
    qkT = P.sb("qkT", [48, 8, 128], BF16)
    vr = P.sb("vr", [128, G, 384], BF16)
    gsil = P.sb("gsil", [128, G, 384], BF16)
    ubuf = P.sb("ubuf", [128, 2, 256], F32)
    state_f = P.sb("state_f", [48, 384], F32)
    state_b = P.sb("state_b", [48, 384], BF16)
    sTb = P.sb("sTb", [128, 4, 128], BF16)
    orn = P.sb("orn", [128, 4, 96], F32)
    mixr = P.sb("mixr", [128, 384], BF16)
    st4 = P.sb("st4", [128, 4, 6], F32); mv4 = P.sb("mv4", [128, 4, 2], F32); rs4 = P.sb("rs4", [128, 4], F32)
    pooledT = P.sb("pooledT", [64, 4, 128], F32)
    pwext = P.sb("pwext", [64, 4, 128], F32)
    pscale = P.sb("pscale", [128, 2], F32)
    gng = P.sb("gng", [128, 384], F32)
    bfb = P.sb("bfb", [128, 6], F32)
    ftmp = P.sb("ftmp", [128, 6], F32); logf = P.sb("logf", [128, 6], F32)
    cneg = P.sb("cneg", [128, nch, 6], F32)
    carry = P.sb("carry", [128, nch + 1, 6], F32)
    crep = P.sb("crep", [128, G, 6, 3], F32)
    c3 = [P.sb("c3_%d" % i, [3, NT], BF16) for i in range(2)]
    c3h = P.sb("c3h", [3, NT], BF16); c3l = P.sb("c3l", [3, NT], BF16)
    c3r = P.sb("c3r", [3, NT], F32); c3s = P.sb("c3s", [3, NT], F32)
    PT = [P.sb("PT%d" % i, [128, NT], BF16) for i in range(3)]
    mixT = P.sb("mixT", [128, 8, NT], BF16)
    gT = P.sb("gT", [128, NFT, NT], BF16)
    stmp = [P.sb("stmp%d" % i, [128, NT], F32) for i in range(2)]
    orn2 = stmp[0][:, 0:384]
    rden = stmp[1]
    xtmp = gT[:, 0:6, :].rearrange("p a b -> p (a b)").bitcast(F32)[:, 0:D]
    hbf = gT[:, 6:9, :].rearrange("p a b -> p (a b)")[:, 0:D]
    qkall = gT[:, 9:15, :].rearrange("p a b -> p (a b)").bitcast(F32).rearrange("p (g c) -> p g c", g=3)
    uall = gT[:, 15:19, :].rearrange("p a b -> p (a b)").bitcast(F32).rearrange("p (g c) -> p g c", g=3)
    GTN = ['gT', 'gTa', 'gTb', 'gTc', 'gTd']
    lnp = [P.sb("lnp%d" % i, [128, 2, D], F32) for i in range(2)]
    lst = P.sb("lst", [128, 2, 6], F32); lmv = P.sb("lmv", [128, 2], F32); lrs = P.sb("lrs", [128, 1], F32)

    mmb = [P.ps("mm%d" % i, [128, 512], F32) for i in range(2)]
    stb = [P.ps("st%d" % i, [128, 512], F32) for i in range(2)]
    oacc = P.ps("oacc", [128, 512], F32)
    dacc = P.ps("dacc", [128, 512], F32)
    tpb = P.ps("tp", [128, 1024], BF16)
    scb = P.ps("sc", [128, 512], F32)
    rot = {'mm': 0, 'st': 0, 'PT': 0, 'stmp': 0, 'c3': 0, 'lnp': 0}

    def nxt(kind, n):
        i = rot[kind]; rot[kind] = (i + 1) % n
        return i

    def mm(out_, lhsT, rhs, start, stop, reads, writes, inc=True):
        P.op('pe', lambda e: e.matmul(out_, lhsT, rhs, start=start, stop=stop), reads, writes, inc)

    def tp(out_, in_, idn, reads, writes, inc=True):
        P.op('pe', lambda e: e.transpose(out_, in_, idn), reads, writes, inc)

    groups = [list(range(g0, min(g0 + G, nch))) for g0 in range(0, nch, G)]
    plan = []
    for l in range(depth):
        for gi in range(len(groups)):
            for c0, c1 in ((0, 384), (384, 768), (768, 1152), (1152, 1408), (2176, 2560), (2560, 2566), (1408, 1792), (1792, 2176)):
                plan.append(('in', l, c0, c1))
            for qtr in range(4):
                plan.append(('out', l, qtr))
            for fb in range(8):
                plan.append(('w1', l, fb)); plan.append(('w3', l, fb))
            for half in range(2):
                for fb in range(4):
                    plan.append(('w2', l, half, fb))
    ws = {'i': 0, 'issued': 0, 'released': [False] * len(plan)}

    def ws_view(key, slot):
        t = wring[slot]
        if key[0] == 'in':
            c = key[3] - key[2]
            return t[:, 0:8 * c].rearrange("p (k c) -> p k c", k=8)
        if key[0] == 'out':
            return t[:, 0:8 * 256].rearrange("p (k c) -> p k c", k=8)
        if key[0] in ('w1', 'w3'):
            nf = min(3, NFT - 3 * key[2])
            return t[:, 0:8 * nf * 128].rearrange("p (k c) -> p k c", k=8)
        nf = min(6, NFT - 6 * key[3])
        return t[:, 0:nf * 512].rearrange("p (k c) -> p k c", k=nf)

    def ws_issue(j):
        key = plan[j]; slot = j % NSLOT
        v = ws_view(key, slot)
        l = key[1]
        if key[0] == 'in':
            src = wbf[('in', l)][:, key[2]:key[3]].rearrange("(k p) c -> p k c", p=128)
            rd = wnames('in', l, 0, D)
        elif key[0] == 'out':
            src = wbf[('out', l)][:, key[2] * 256:(key[2] + 1) * 256].rearrange("(k p) c -> p k c", p=128)
            rd = wnames('out', l, 0, D)
        elif key[0] in ('w1', 'w3'):
            nf = min(3, NFT - 3 * key[2])
            src = wbf[(key[0], l)][:, key[2] * 384:key[2] * 384 + nf * 128].rearrange("(k p) c -> p k c", p=128)
            rd = wnames(key[0], l, 0, D)
        else:
            nf = min(6, NFT - 6 * key[3])
            r0 = key[3] * 6 * 128
            src = wbf[('w2', l)][r0:r0 + nf * 128, key[2] * 512:(key[2] + 1) * 512].rearrange("(k p) c -> p k c", p=128)
            rd = wnames('w2', l, r0, r0 + nf * 128)
        P.dma('sp', v, src, reads=rd, writes=['wr%d' % slot])

    def ws_pump():
        while ws['issued'] < len(plan) and ws['issued'] < ws['i'] + NSLOT:
            j = ws['issued']
            if j - NSLOT >= 0 and not ws['released'][j - NSLOT]:
                break
            ws_issue(j); ws['issued'] += 1

    def ws_next(kind):
        ws_pump()
        j = ws['i']
        assert plan[j][0] == kind, (plan[j], kind)
        assert ws['issued'] > j
        ws['i'] += 1
        slot = j % NSLOT
        return j, ws_view(plan[j], slot), 'wr%d' % slot

    def ws_done(j):
        ws['released'][j] = True
        ws_pump()

    def layernorm(xap, xname, gap, bap, gbname):
        for c in range(2):
            P.op('dve', lambda e, c=c: e.bn_stats(lst[:, c, :], xap[:, c * 512:(c + 1) * 512]), [xname], ['lst'])
        P.op('dve', lambda e: e.bn_aggr(lmv[:], lst[:]), ['lst'], ['lmv'])
        P.op('act', lambda e: e.activation(lrs[:], lmv[:, 1:2], AF.Ln, bias=LN_EPS), ['lmv'], ['lrs'])
        P.op('act', lambda e: e.activation(lrs[:], lrs[:], AF.Exp, scale=-0.5), ['lrs'], ['lrs'])
        P.op('dve', lambda e: e.tensor_scalar(xap, xap, lmv[:, 0:1], lrs[:, 0:1], ALU.subtract, ALU.mult),
             [xname, 'lmv', 'lrs'], [xname])
        P.op('dve', lambda e: e.tensor_tensor(xap, xap, gap, ALU.mult), [xname, gbname], [xname])
        P.op('pool', lambda e: e.tensor_tensor(xap, xap, bap, ALU.add), [xname, gbname], [xname])

    def load_ln(gvec, bvec):
        i = nxt('lnp', 2)
        nm = 'lnp%d' % i
        P.dma('sp', lnp[i][:, 0, :], gvec.partition_broadcast(128), writes=[nm])
        P.dma('sp', lnp[i][:, 1, :], bvec.partition_broadcast(128), writes=[nm])
        return lnp[i][:, 0, :], lnp[i][:, 1, :], nm

    def to_hT(ci, n, zero_pad):
        hn = 'hres%d' % ci
        P.op('pool', lambda e: e.tensor_copy(hbf[:], hres[:, ci, :]), [hn], ['gTb'])
        tv = tpb[:, :].rearrange("p (k t) -> p k t", k=8)
        for k in range(8):
            tp(tv[:, k, :], hbf[:, k * 128:(k + 1) * 128], ident_b, ['gTb', 'cbf'], ['tp'], inc=(k == 7))
        P.op('act', lambda e: e.copy(hT[:, :, ci * 128:(ci + 1) * 128], tv), ['tp'], ['hT%d' % ci])
        if zero_pad:
            cm = cmask_b[:, n * 128:(n + 1) * 128].unsqueeze(1).broadcast_to([128, 8, 128])
            P.op('pool', lambda e: e.tensor_tensor(hT[:, :, ci * 128:(ci + 1) * 128], hT[:, :, ci * 128:(ci + 1) * 128], cm, ALU.mult),
                 ['hT%d' % ci, 'cmask_b'], ['hT%d' % ci])

    for l in range(depth):
        P.dma('sp', gng[:], ret_gn_g[l].partition_broadcast(128), writes=['gng'])
        P.dma('sp', bfb[:], b_f[l].partition_broadcast(128), writes=['bfb'])
        P.op('dve', lambda e: e.memset(pwext[:], 0.0), [], ['pwext'])
        for g in range(4):
            h0 = (g % 2) * 64
            P.dma('sp', pwext[:, g, h0:h0 + 64], pool_w[l, g], reads=[], writes=['pwext'])
        for r in range(2):
            P.dma('sp', pscale[:, r:r + 1], pool_scale[l, r * 128:(r + 1) * 128].rearrange("(p o) -> p o", o=1), writes=['pscale'])
        P.op('dve', lambda e: e.memset(state_f[:], 0.0), [], ['state_f'])
        P.op('dve', lambda e: e.memset(state_b[:], 0.0), [], ['state_b'])
        P.op('dve', lambda e: e.memset(carry[:, 0, :], 0.0), [], ['carry'])
        P.op('dve', lambda e: e.memset(hres[:, 0, :], 0.0), [], ['hres0'])
        for ci in range(G):
            P.dma('sp', obuf[ci * 128:(ci + 1) * 128, :], hres[:, 0, :], reads=['hres0'], writes=['obuf'])
        fA = flags[:, 0:1]; fB = flags[:, 1:2]

        for grp in groups:
            Gc = len(grp); N = Gc * 128; n0 = grp[0]; n1 = grp[-1] + 1
            hTn = ['hT%d' % ci for ci in range(Gc)]
            P.stage = 'A'
            P.cc("AllGather", [list(pr_) for pr_ in pairs], obuf, gath, reads=['obuf'], writes=['gath'])
            eg, eb, enm = load_ln(ln_emb_g, ln_emb_b)
            for ci, n in enumerate(grp):
                hn = 'hres%d' % ci
                hv = hres[:, ci, :]
                P.dma('sp', hv, gath[ci * 128:(ci + 1) * 128, :], reads=['gath'], writes=[hn])
                P.dma('sp', xtmp, xin[n * 128:(n + 1) * 128, :], writes=['gTa'])
                P.op('dve', lambda e: e.tensor_scalar(hv, hv, fB, None, ALU.mult), [hn, 'cst'], [hn])
                P.op('dve', lambda e: e.scalar_tensor_tensor(hv, xtmp, fA, hv, ALU.mult, ALU.add), [hn, 'gTa', 'cst'], [hn])
                P.op('pool', lambda e: e.tensor_copy(xtmp, hv), [hn], ['gTa'])
                layernorm(hv, hn, eg, eb, enm)
                P.op('dve', lambda e: e.tensor_tensor(hv, hv, xtmp, ALU.subtract), [hn, 'gTa'], [hn])
                P.op('dve', lambda e: e.scalar_tensor_tensor(hv, hv, fA, xtmp, ALU.mult, ALU.add), [hn, 'gTa', 'cst'], [hn])
                to_hT(ci, n, zero_pad=(n <= G))
            P.stage = 'B'
            for bname in ('qk', 'vr', 'gr', 'up', 'vf', 'f'):
                j, wv, wn = ws_next('in')
                ncol = plan[j][3] - plan[j][2]
                for ci, n in enumerate(grp):
                    b = nxt('mm', 2); pm = mmb[b]; pn = 'mm%d' % b
                    for k in range(8):
                        mm(pm[:, 0:ncol], hT[:, k, ci * 128:(ci + 1) * 128], wv[:, k, :], k == 0, k == 7,
                           [hTn[ci], wn], [pn], inc=(k == 7))
                    if bname == 'qk':
                        P.op('act', lambda e, pm=pm, ci=ci: e.copy(qkall[:, ci, :], pm[:, 0:384]), [pn], ['gTc'])
                    elif bname == 'vr':
                        P.op('act', lambda e, pm=pm, ci=ci: e.copy(vr[:, ci, :], pm[:, 0:384]), [pn], ['vr%d' % ci])
                    elif bname == 'gr':
                        P.op('act', lambda e, pm=pm, ci=ci: e.activation(gsil[:, ci, :], pm[:, 0:384], AF.Silu), [pn], ['gsil%d' % ci])
                    elif bname == 'up':
                        P.op('dve', lambda e, pm=pm, ci=ci: e.tensor_copy(uall[:, ci, :], pm[:, 0:256]), [pn], ['gTd'])
                    elif bname == 'vf':
                        P.op('act', lambda e, pm=pm, n=n: e.copy(Vst[:, n, :], pm[:, 0:384]), [pn], ['Vst%d' % n])
                    else:
                        P.op('dve', lambda e, pm=pm, ci=ci: e.tensor_tensor(fall[:, ci, :], pm[:, 0:6], bfb[:], ALU.add),
                             [pn, 'bfb'], ['fall%d' % ci])
                ws_done(j)
            for which in ('qf', 'kf'):
                j, wv, wn = ws_next('in')
                for mt in range(3):
                    b = nxt('mm', 2); pm = mmb[b]; pn = 'mm%d' % b
                    for k in range(8):
                        mm(pm[:, 0:N], wv[:, k, mt * 128:(mt + 1) * 128], hT[:, k, 0:N], k == 0, k == 7,
                           hTn + [wn], [pn], inc=(k == 7))
                    if which == 'qf':
                        P.op('act', lambda e, pm=pm, mt=mt: e.mul(qT[:, mt, 0:N], pm[:, 0:N], 0.125), [pn], ['qT'])
                    else:
                        P.op('act', lambda e, pm=pm, mt=mt: e.copy(KT[:, mt, n0 * 128:n0 * 128 + N], pm[:, 0:N]), [pn], ['KT%d' % (n0 // G)])
                ws_done(j)
            KTn = ['KT%d' % gi for gi in range(n0 // G + 1)]

            P.stage = 'C'
            for ci, n in enumerate(grp):
                ccols = slice(ci * 128, (ci + 1) * 128)
                qkv = qkall[:, ci, :].rearrange("p (h d) -> p h d", h=8)
                x1 = qkv[:, :, 0:24]; x2 = qkv[:, :, 24:48]
                cb = cos[:, n, :].unsqueeze(1).broadcast_to([128, 8, 24])
                sb_ = sin[:, n, :].unsqueeze(1).broadcast_to([128, 8, 24])
                qn = 'gTc'
                P.op('dve', lambda e: e.tensor_tensor(rt[0][:], x1, cb, ALU.mult), [qn, 'cst'], ['rt0'])
                P.op('pool', lambda e: e.tensor_tensor(rt[1][:], x2, sb_, ALU.mult), [qn, 'cst'], ['rt1'])
                P.op('dve', lambda e: e.tensor_tensor(rt[2][:], x1, sb_, ALU.mult), [qn, 'cst'], ['rt2'])
                P.op('pool', lambda e: e.tensor_tensor(rt[3][:], x2, cb, ALU.mult), [qn, 'cst'], ['rt3'])
                P.op('dve', lambda e: e.tensor_tensor(rr[:, :, 0:24], rt[0][:], rt[1][:], ALU.subtract), ['rt0', 'rt1'], ['rr'])
                P.op('dve', lambda e: e.tensor_tensor(rr[:, :, 24:48], rt[2][:], rt[3][:], ALU.add), ['rt2', 'rt3'], ['rr'])
                P.op('dve', lambda e: e.tensor_tensor(qkr[:], rr[:], dec.unsqueeze(2).broadcast_to([128, 8, 48]), ALU.mult),
                     ['rr', 'cst'], ['qkr'])
                tq = tpb[0:48, :].rearrange("p (h t) -> p h t", h=8)
                for h in range(8):
                    tp(tq[:, h, :], qkr[:, h, :], ident_b, ['qkr', 'cbf'], ['tp'], inc=(h == 7))
                P.op('act', lambda e: e.copy(qkT[:], tq), ['tp'], ['qkT'])
                scv = scb[:, :].rearrange("p (h t) -> p h t", h=4)
                for h in range(4):
                    mm(scv[:, h, :], qkT[:, 4 + h, :], qkT[:, h, :], True, True, ['qkT'], ['sc'], inc=(h == 3))
                P.op('dve', lambda e: e.tensor_tensor(sTb[:], scv, triu_f.unsqueeze(1).broadcast_to([128, 4, 128]), ALU.mult),
                     ['sc', 'cst'], ['sTb'])
                b = nxt('mm', 2); pm = mmb[b]; pn = 'mm%d' % b
                ov = pm[:, 0:384].rearrange("p (h e) -> p h e", h=4)
                stv = state_b[:, :].rearrange("p (h e) -> p h e", h=4)
                for h in range(4):
                    mm(ov[:, h, :], sTb[:, h, :], vr[:, ci, h * 96:(h + 1) * 96], True, False, ['sTb', 'vr%d' % ci], [pn], inc=False)
                    mm(ov[:, h, :], qkT[:, h, :], stv[:, h, :], False, True, ['qkT', 'state_b'], [pn], inc=(h == 3))
                for h in range(4):
                    P.op('dve', lambda e, h=h: e.bn_stats(st4[:, h, :], ov[:, h, :]), [pn], ['st4'])
                for h in range(4):
                    P.op('dve', lambda e, h=h: e.bn_aggr(mv4[:, h, :], st4[:, h, :]), ['st4'], ['mv4'])
                P.op('act', lambda e: e.activation(rs4[:], mv4[:, :, 1], AF.Ln, bias=LN_EPS), ['mv4'], ['rs4'])
                P.op('act', lambda e: e.activation(rs4[:], rs4[:], AF.Exp, scale=-0.5), ['rs4'], ['rs4'])
                for h in range(4):
                    P.op('dve', lambda e, h=h: e.tensor_scalar(orn[:, h, :], ov[:, h, :], mv4[:, h, 0:1], rs4[:, h:h + 1],
                                                               ALU.subtract, ALU.mult), [pn, 'mv4', 'rs4'], ['orn'])
                P.op('pool', lambda e: e.tensor_tensor(orn2[:], orn[:, :, :].rearrange("p h e -> p (h e)"), gng[:], ALU.mult),
                     ['orn', 'gng'], ['stmp0'])
                P.op('dve', lambda e: e.tensor_tensor(mixr[:], orn2[:], gsil[:, ci, :], ALU.mult), ['stmp0', 'gsil%d' % ci], ['mixr'])
                b = nxt('mm', 2); pk = mmb[b]; pkn = 'mm%d' % b
                for h in range(4):
                    mm(pk[0:48, h * 96:(h + 1) * 96], qkr[:, 4 + h, :], vr[:, ci, h * 96:(h + 1) * 96], True, True,
                       ['qkr', 'vr%d' % ci], [pkn], inc=(h == 3))
                P.op('dve', lambda e: e.tensor_tensor(state_f[:], state_f[:], pk[0:48, 0:384], ALU.add), ['state_f', pkn], ['state_f'])
                P.op('dve', lambda e: e.tensor_tensor(state_f[:], state_f[:], gtile[0:48, :], ALU.mult), ['state_f', 'cst'], ['state_f'])
                P.op('dve', lambda e: e.tensor_copy(state_b[:], state_f[:]), ['state_f'], ['state_b'])
                tm = tpb[:, 0:384].rearrange("p (k t) -> p k t", k=3)
                for k in range(3):
                    tp(tm[:, k, :], mixr[:, k * 128:(k + 1) * 128], ident_b, ['mixr', 'cbf'], ['tp'], inc=(k == 2))
                P.op('act', lambda e: e.copy(mixT[:, 0:3, ccols], tm), ['tp'], ['mixT%d' % ci])

                us = n % 2
                P.op('pool', lambda e: e.tensor_copy(ubuf[:, us, :], uall[:, ci, :]), ['gTd'], ['ubuf%d' % us])
                b = nxt('mm', 2); pp = mmb[b]; ppn = 'mm%d' % b
                ppv = pp[0:64, :].rearrange("p (g t) -> p g t", g=4)
                for g in range(4):
                    if n == 0:
                        mm(ppv[:, g, :], ubuf[:, us, g * 64:(g + 1) * 64], poolX[:, g, :], True, True,
                           ['ubuf%d' % us, 'cst'], [ppn], inc=(g == 3))
                    else:
                        mm(ppv[:, g, :], ubuf[:, us, g * 64:(g + 1) * 64], (poolY if n == G else poolA)[:, g, :], True, False,
                           ['ubuf%d' % us, 'cst'], [ppn], inc=False)
                        mm(ppv[:, g, :], ubuf[:, 1 - us, g * 64:(g + 1) * 64], poolB[:, g, :], False, True,
                           ['ubuf%d' % (1 - us), 'cst'], [ppn], inc=(g == 3))
                P.op('act', lambda e: e.copy(pooledT[:], ppv), [ppn], ['pooledT'])
                b = nxt('mm', 2); py = mmb[b]; pyn = 'mm%d' % b
                pyv = py[:, 0:256].rearrange("p (r t) -> p r t", r=2)
                for g in range(4):
                    mm(pyv[:, g // 2, :], pwext[:, g, :], pooledT[:, g, :], g % 2 == 0, g % 2 == 1,
                       ['pwext', 'pooledT'], [pyn], inc=(g == 3))
                for r in range(2):
                    P.op('dve', lambda e, r=r: e.tensor_scalar(mixT[:, 3 + r, ccols], pyv[:, r, :], pscale[:, r:r + 1], None, ALU.mult),
                         [pyn, 'pscale'], ['mixT%d' % ci])

                P.op('act', lambda e: e.activation(ftmp[:], fall[:, ci, :], AF.Exp, scale=-1.0), ['fall%d' % ci], ['ftmp'])
                P.op('act', lambda e: e.activation(ftmp[:], ftmp[:], AF.Ln, bias=1.0), ['ftmp'], ['ftmp'])
                if n <= G:
                    P.op('dve', lambda e: e.tensor_scalar(logf[:], ftmp[:], -1.0, vmaskt[:, n:n + 1], ALU.mult, ALU.mult), ['ftmp', 'cst'], ['logf'])
                else:
                    P.op('dve', lambda e: e.tensor_scalar(logf[:], ftmp[:], -1.0, None, ALU.mult), ['ftmp'], ['logf'])
                b = nxt('mm', 2); pc = mmb[b]; pcn = 'mm%d' % b
                mm(pc[:, 0:6], triu_f, logf[:], True, True, ['cst', 'logf'], [pcn], inc=False)
                mm(pc[:, 6:12], ones_f, logf[:], True, True, ['cst', 'logf'], [pcn], inc=True)
                P.op('dve', lambda e: e.scalar_tensor_tensor(cneg[:, n, :], pc[:, 0:6], -1.0, carry[:, n, :], ALU.mult, ALU.subtract),
                     [pcn, 'carry'], ['cneg'])
                if n <= G:
                    P.op('dve', lambda e: e.tensor_scalar(cneg[:, n, :], cneg[:, n, :], padnegt[:, n:n + 1], None, ALU.add), ['cneg', 'cst'], ['cneg'])
                P.op('dve', lambda e: e.tensor_tensor(crep[:, ci, :, :], pc[:, 0:6].unsqueeze(2).broadcast_to([128, 6, 3]),
                                                      carry[:, n, :].unsqueeze(2).broadcast_to([128, 6, 3]), ALU.add),
                     [pcn, 'carry'], ['crep%d' % ci])
                P.op('dve', lambda e: e.tensor_tensor(carry[:, n + 1, :], carry[:, n, :], pc[:, 6:12], ALU.add), [pcn, 'carry'], ['carry'])

            P.stage = 'D'
            for h in range(6):
                pr = h // 2; r0 = (h % 2) * 64
                b = nxt('mm', 2); px = mmb[b]; pxn = 'mm%d' % b
                for ci in range(Gc):
                    mm(px[0:3, ci * 128:(ci + 1) * 128], crep[:, ci, h, :], ident_f, True, True, ['crep%d' % ci, 'cst'], [pxn], inc=(ci == Gc - 1))
                k3 = nxt('c3', 2); c3t = c3[k3]; c3n = 'c3_%d' % k3
                P.op('dve', lambda e: e.tensor_copy(c3h[:, 0:N], px[0:3, 0:N]), [pxn], ['c3h'])
                P.op('dve', lambda e: e.tensor_tensor(c3r[:, 0:N], px[0:3, 0:N], c3h[:, 0:N], ALU.subtract), [pxn, 'c3h'], ['c3r'])
                P.op('dve', lambda e: e.tensor_copy(c3l[:, 0:N], c3r[:, 0:N]), ['c3r'], ['c3l'])
                P.op('dve', lambda e: e.tensor_tensor(c3r[:, 0:N], c3r[:, 0:N], c3l[:, 0:N], ALU.subtract), ['c3r', 'c3l'], ['c3r'])
                P.op('dve', lambda e: e.tensor_scalar(c3s[:, 0:N], c3h[:, 0:N], e3[0:3, 0:1], None, ALU.mult), ['c3h', 'cst'], ['c3s'])
                P.op('dve', lambda e: e.scalar_tensor_tensor(c3s[:, 0:N], c3l[:, 0:N], e3[0:3, 1:2], c3s[:, 0:N], ALU.mult, ALU.add), ['c3l', 'c3s', 'cst'], ['c3s'])
                P.op('dve', lambda e: e.scalar_tensor_tensor(c3t[:, 0:N], c3r[:, 0:N], e3[0:3, 2:3], c3s[:, 0:N], ALU.mult, ALU.add), ['c3r', 'c3s', 'cst'], [c3n])
                def emit_S(J):
                    ql = max(J, n0) - n0
                    c0 = ql * 128
                    si = nxt('st', 2); sp_ = stb[si]; spn = 'st%d' % si
                    mm(sp_[:, c0:N], KT[r0:r0 + 64, pr, J * 128:(J + 1) * 128], qT[r0:r0 + 64, pr, c0:N], True, False,
                       [KTn[J // G], 'qT'], [spn], inc=False)
                    mm(sp_[:, c0:N], ones_b[0:3, :], c3t[:, c0:N], False, True, ['cbf', c3n], [spn])
                    return sp_, spn

                def emit_rest(J, sp_, spn):
                    ql = max(J, n0) - n0
                    c0 = ql * 128
                    pi = nxt('PT', 3); pt_ = PT[pi]; ptn = 'PT%d' % pi
                    P.op('act', lambda e, J=J, pt_=pt_, sp_=sp_: e.activation(
                        pt_[:, c0:N], sp_[:, c0:N], AF.Exp, bias=cneg[:, J, h:h + 1], scale=1.0),
                        [spn, 'cneg'], [ptn])
                    if J >= n0:
                        P.op('pool', lambda e, pt_=pt_, c0=c0: e.tensor_tensor(pt_[:, c0:c0 + 128], pt_[:, c0:c0 + 128], triu_b, ALU.mult),
                             [ptn, 'cbf'], [ptn])
                    mm(oacc[:, c0:N], Vst[:, J, pr * 128:(pr + 1) * 128], pt_[:, c0:N], J == 0, J == n1 - 1,
                       ['Vst%d' % J, ptn], ['oacc'], inc=False)
                    mm(dacc[:, c0:N], ones_b, pt_[:, c0:N], J == 0, J == n1 - 1, ['cbf', ptn], ['dacc'], inc=True)

                cur = emit_S(0)
                for J in range(n1):
                    nxt_s = emit_S(J + 1) if J + 1 < n1 else None
                    emit_rest(J, *cur)
                    cur = nxt_s
                P.op('dve', lambda e: e.tensor_scalar(rden[:, 0:N], dacc[:, 0:N], 1e-30, None, ALU.max), ['dacc'], ['stmp1'])
                P.op('dve', lambda e: e.reciprocal(rden[:, 0:N], rden[:, 0:N]), ['stmp1'], ['stmp1'])
                P.op('dve', lambda e, pr=pr, r0=r0: e.tensor_tensor(mixT[r0:r0 + 64, 5 + pr, 0:N], oacc[r0:r0 + 64, 0:N], rden[r0:r0 + 64, 0:N], ALU.mult),
                     ['oacc', 'stmp1'], ['mixT%d' % ci for ci in range(Gc)])

            P.stage = 'E'
            g1, b1, n1m = load_ln(ln1_g[l], ln1_b[l])
            for qtr in range(4):
                j, wv, wn = ws_next('out')
                for ci, n in enumerate(grp):
                    b = nxt('mm', 2); pm = mmb[b]; pn = 'mm%d' % b
                    for k in range(8):
                        mm(pm[:, 0:256], mixT[:, k, ci * 128:(ci + 1) * 128], wv[:, k, :], k == 0, k == 7,
                           ['mixT%d' % ci, wn], [pn], inc=(k == 7))
                    hv = hres[:, ci, qtr * 256:(qtr + 1) * 256]
                    P.op('dve', lambda e, hv=hv, pm=pm: e.scalar_tensor_tensor(hv, hv, ALPHA, pm[:, 0:256], ALU.mult, ALU.add),
                         ['hres%d' % ci, pn], ['hres%d' % ci])
                ws_done(j)
            for ci, n in enumerate(grp):
                layernorm(hres[:, ci, :], 'hres%d' % ci, g1, b1, n1m)
                to_hT(ci, n, zero_pad=False)

            P.stage = 'F'
            for fb in range(8):
                j1, w1v, w1n = ws_next('w1')
                j3, w3v, w3n = ws_next('w3')
                nf = min(3, NFT - 3 * fb)
                for fi in range(nf):
                    ft = fb * 3 + fi
                    b = nxt('mm', 2); pa = mmb[b]; pan = 'mm%d' % b
                    for k in range(8):
                        mm(pa[:, 0:N], w1v[:, k, fi * 128:(fi + 1) * 128], hT[:, k, 0:N], k == 0, k == 7, hTn + [w1n], [pan], inc=(k == 7))
                    si = nxt('st', 2); pb = stb[si]; pbn = 'st%d' % si
                    for k in range(8):
                        mm(pb[:, 0:N], w3v[:, k, fi * 128:(fi + 1) * 128], hT[:, k, 0:N], k == 0, k == 7, hTn + [w3n], [pbn], inc=(k == 7))
                    ti = nxt('stmp', 2); tt = stmp[ti]; ttn = 'stmp%d' % ti
                    P.op('act', lambda e, pa=pa, tt=tt: e.activation(tt[:, 0:N], pa[:, 0:N], AF.Silu), [pan], [ttn])
                    P.op('dve', lambda e, pb=pb, tt=tt, ft=ft: e.tensor_tensor(gT[:, ft, 0:N], tt[:, 0:N], pb[:, 0:N], ALU.mult), [ttn, pbn], GTN)
                ws_done(j1); ws_done(j3)
            g2, b2, n2m = load_ln(ln2_g[l], ln2_b[l])
            for half in range(2):
                blocks = [ws_next('w2') for _ in range(4)]
                for ci, n in enumerate(grp):
                    b = nxt('mm', 2); pm = mmb[b]; pn = 'mm%d' % b
                    for ft in range(NFT):
                        j, wv, wn = blocks[ft // 6]
                        mm(pm[:, :], gT[:, ft, ci * 128:(ci + 1) * 128], wv[:, ft % 6, :], ft == 0, ft == NFT - 1,
                           GTN + [wn], [pn], inc=(ft == NFT - 1))
                    hv = hres[:, ci, half * 512:(half + 1) * 512]
                    P.op('dve', lambda e, hv=hv, pm=pm: e.scalar_tensor_tensor(hv, hv, ALPHA, pm[:, :], ALU.mult, ALU.add),
                         ['hres%d' % ci, pn], ['hres%d' % ci])
                for j, _, _ in blocks:
                    ws_done(j)
            for ci, n in enumerate(grp):
                layernorm(hres[:, ci, :], 'hres%d' % ci, g2, b2, n2m)
                if n >= G + 1 and n - G - 1 < nch_seq - 1:
                    r_ = (n - G - 1) * 128
                    P.dma('sp', out[r_:r_ + 128, :], hres[:, ci, :], reads=['hres%d' % ci], writes=['out%d' % n])
                P.dma('sp', obuf[ci * 128:(ci + 1) * 128, :], hres[:, ci, :], reads=['hres%d' % ci], writes=['obuf'])
    P.finish(['out'])
    return nc, P


def make_in_maps(inputs, G=3):
    x = np.ascontiguousarray(np.asarray(inputs['x'], dtype=np.float32))
    B, S, _ = x.shape
    nch = S // 128 + 1
    NP = nch + G
    f32 = lambda a: np.ascontiguousarray(np.asarray(a, dtype=np.float32))
    meta = f32(inputs['meta'])
    per_layer = ('w_in', 'b_f', 'ret_gn_g', 'pool_w', 'pool_scale', 'w_out', 'ln1_g', 'ln1_b',
                 'w_ffn1', 'w_ffn3', 'w_ffn2', 'ln2_g', 'ln2_b')
    consts = [host_consts(nch, G, r) for r in range(2)]
    in_maps = []
    for c in range(2 * B):
        role = c // B
        b = c % B
        m = {'ln_emb_g': f32(inputs['ln_emb_g']), 'ln_emb_b': f32(inputs['ln_emb_b'])}
        for k in per_layer:
            m[k] = f32(np.asarray(inputs[k])[role:role + 1])
        xin = np.zeros((NP * 128, D), np.float32)
        if role == 0:
            xin[PADN:128] = meta
            xin[128:128 + S] = x[b]
        m['xin'] = xin
        m['consts'], m['colmask'] = consts[role]
        in_maps.append(m)
    return in_maps, B, nch


def kernel(**inputs):
    G = 3
    in_maps, B, nch = make_in_maps(inputs, G)
    nc, _ = build_program(nch, G, pairs=tuple((b, B + b) for b in range(B)))
    res = run_bass_kernel_spmd(nc, in_maps, core_ids=list(range(2 * B)))
    return np.stack([np.asarray(res.results[B + b]['out'], dtype=np.float32) for b in range(B)], axis=0)
```

```python
import numpy as np
import concourse.bass as bass
import concourse.mybir as mybir
from concourse.bass_utils import run_bass_kernel_spmd

F32 = mybir.dt.float32
BF16 = mybir.dt.bfloat16
AF = mybir.ActivationFunctionType
ALU = mybir.AluOpType


class _Rec:
    def __init__(self):
        self.call = None

    def __getattr__(self, name):
        def f(*a, **k):
            self.call = (name, a, k)
            return self
        return f


def _bind(fn):
    rec = _Rec()
    fn(rec)
    name, a, k = rec.call
    return lambda e: getattr(e, name)(*a, **k)


class Prog:
    ENGS = ('pe', 'act', 'dve', 'pool', 'sp')
    NDMA = 16

    def __init__(self, nc):
        self.nc = nc
        self.ops = {e: [] for e in self.ENGS}
        self.sem = {}
        for e in self.ENGS:
            self.sem['c_' + e] = nc.alloc_semaphore('c_' + e)
        self.dma_sems = {}
        for q in ('sp', 'pool', 'act'):
            self.dma_sems[q] = []
            for i in range(self.NDMA if q == 'sp' else 8):
                nm = 'd%s%d' % (q, i)
                self.sem[nm] = nc.alloc_semaphore(nm)
                self.dma_sems[q].append(nm)
        self.cnt = {s: 0 for s in self.sem}
        self.known = {e: {} for e in self.ENGS}
        self.lastw = {}
        self.readers = {}
        self.pending = {e: [] for e in self.ENGS}
        self.dma_rr = {'sp': 0, 'pool': 0, 'act': 0}
        self.n_ops = 0
        self.stage = ''
        self.tags = {e: [] for e in self.ENGS}

    def sb(self, name, shape, dtype):
        return self.nc.alloc_sbuf_tensor(name, list(shape), dtype)

    def ps(self, name, shape, dtype):
        return self.nc.alloc_psum_tensor(name, list(shape), dtype)

    def _collect(self, reads, writes):
        deps = []
        for r in reads:
            t = self.lastw.get(r)
            if t is not None:
                deps.append(t)
        for w in writes:
            t = self.lastw.get(w)
            if t is not None:
                deps.append(t)
            deps.extend(self.readers.get(w, {}).values())
        return deps

    def _waits(self, eng, deps):
        need = {}
        own = 'c_' + eng
        for sem, val in deps:
            if eng == 'pe' and sem == own:
                continue
            if val is None:
                raise RuntimeError("dependency on unresolved token %s from %s" % (sem, eng))
            if self.known[eng].get(sem, 0) >= val:
                continue
            if need.get(sem, 0) < val:
                need[sem] = val
        for s, v in need.items():
            self.known[eng][s] = v
        return list(need.items())

    def _track(self, tok, reads, writes):
        ws = set(writes)
        for r in reads:
            if r in ws:
                continue
            self.readers.setdefault(r, {})[tok[0]] = tok
        for w in writes:
            self.lastw[w] = tok
            self.readers[w] = {}

    def op(self, eng, fn, reads=(), writes=(), inc=True):
        sem = 'c_' + eng
        tok = [sem, None]
        waits = self._waits(eng, self._collect(reads, writes))
        if inc:
            self.cnt[sem] += 1
            tok[1] = self.cnt[sem]
            for p in self.pending[eng]:
                p[1] = tok[1]
            self.pending[eng] = []
        else:
            self.pending[eng].append(tok)
        self.ops[eng].append((waits, _bind(fn), (sem, 1) if inc else None))
        self.tags[eng].append(self.stage)
        self._track(tok, reads, writes)
        self.n_ops += 1

    def dma(self, eng, out, in_, reads=(), writes=()):
        pool = self.dma_sems[eng]
        i = self.dma_rr[eng]
        self.dma_rr[eng] = (i + 1) % len(pool)
        sem = pool[i]
        deps = self._collect(reads, writes)
        if self.cnt[sem] > 0:
            deps.append([sem, self.cnt[sem]])
        waits = self._waits(eng, deps)
        self.cnt[sem] += 16
        tok = [sem, self.cnt[sem]]
        self.ops[eng].append((waits, lambda e: e.dma_start(out=out, in_=in_), (sem, 16)))
        self.tags[eng].append(self.stage)
        self._track(tok, reads, writes)
        self.n_ops += 1

    def cc(self, kind, groups, in_ap, out_ap, reads=(), writes=()):
        if 'cc' not in self.sem:
            self.sem['cc'] = self.nc.alloc_semaphore('cc')
            self.cnt['cc'] = 0
        waits = self._waits('pool', self._collect(reads, writes))
        self.cnt['cc'] += 1
        tok = ['cc', self.cnt['cc']]
        self.ops['pool'].append((waits, lambda e: e.collective_compute(kind, ALU.bypass, replica_groups=groups,
                                                                        ins=[in_ap], outs=[out_ap]), ('cc', 1)))
        self.tags['pool'].append(self.stage)
        self._track(tok, reads, writes)
        self.n_ops += 1

    def finish(self, out_bufs):
        deps = []
        for b in out_bufs:
            t = self.lastw.get(b)
            if t is not None:
                deps.append(t)
        for pool in self.dma_sems.values():
            for s in pool:
                if self.cnt[s] > 0:
                    deps.append([s, self.cnt[s]])
        if self.cnt.get('cc', 0) > 0:
            deps.append(['cc', self.cnt['cc']])
        waits = self._waits('sp', deps)
        self.ops['sp'].append((waits, None, None))
        self._emit()

    def _replay(self, name, eng):
        for waits, fn, inc in self.ops[name]:
            for s, v in waits:
                eng.wait_ge(self.sem[s], v)
            if fn is None:
                continue
            ins = fn(eng)
            if inc is not None:
                ins.then_inc(self.sem[inc[0]], inc[1])

    def _emit(self):
        with self.nc.Block() as block:
            @block.tensor
            def _(e):
                self._replay('pe', e)

            @block.scalar
            def _(e):
                self._replay('act', e)

            @block.vector
            def _(e):
                self._replay('dve', e)

            @block.gpsimd
            def _(e):
                self._replay('pool', e)

            @block.sync
            def _(e):
                self._replay('sp', e)


D = 1024
D_IN = 2566
D_FF = 2816
NFT = D_FF // 128
PADN = 112
ALPHA = 4.0 ** 0.25
LN_EPS = 1e-5
GAMMAS = [1.0 - 2.0 ** (-5.0 - h) for h in range(4)]
POOL_WINDOWS = (2, 4, 8, 16)
SLOT = 3072
NSLOT = 6


def host_consts(nch, G, role):
    NP = nch + G
    sh = G if role == 1 else 0
    p = np.arange(128)
    ident = np.eye(128, dtype=np.float32)
    triu = (p[:, None] <= p[None, :]).astype(np.float32)
    ones = np.ones((128, 128), np.float32)
    poolA = np.zeros((128, 4, 128), np.float32)
    poolB = np.zeros((128, 4, 128), np.float32)
    poolC = np.zeros((128, 4, 128), np.float32)
    for g, w in enumerate(POOL_WINDOWS):
        for t in range(128):
            for s_ in range(t - w + 1, t + 1):
                if s_ >= 0:
                    poolA[s_, g, t] += 1.0 / w
                else:
                    poolB[128 + s_, g, t] += 1.0 / w
            poolA[t, g, t] -= 1.0
        for tr in range(16):
            lo = max(tr + 1 - w, 0)
            cnt = tr + 1 - lo
            for sr in range(lo, tr + 1):
                poolC[PADN + sr, g, PADN + tr] += 1.0 / cnt
            poolC[PADN + tr, g, PADN + tr] -= 1.0
    poolX = poolC if role == 0 else poolA
    poolY = poolA if role == 0 else poolC
    dec = np.zeros((128, 8), np.float64)
    for h in range(4):
        dec[:, h] = GAMMAS[h] ** (p + 1.0)
        dec[:, 4 + h] = GAMMAS[h] ** (-(p + 1.0)) * 48.0 ** -0.5
    vm = np.ones((128, G + 1), np.float32)
    pn = np.zeros((128, G + 1), np.float32)
    for n in range(G + 1):
        ln_ = n - sh
        if ln_ < 0:
            vm[:, n] = 0.0; pn[:, n] = -1e30
        elif ln_ == 0:
            vm[:, n] = (p >= PADN); pn[:, n] = np.where(p >= PADN, 0.0, -1e30)
    colmask = np.repeat(vm.T.reshape(1, -1), 128, axis=0).astype(np.float32)
    pos = (np.arange(NP * 128, dtype=np.float32) - np.float32(PADN + sh * 128))
    inv_freq = (np.float32(10000.0) ** (-np.arange(24, dtype=np.float32) / np.float32(24))).astype(np.float32)
    ang = (pos[:, None] * inv_freq[None, :]).astype(np.float32)
    cos = np.cos(ang).astype(np.float32).reshape(NP, 128, 24).transpose(1, 0, 2)
    sin = np.sin(ang).astype(np.float32).reshape(NP, 128, 24).transpose(1, 0, 2)
    gt = np.zeros((128, 4, 96), np.float32)
    for h in range(4):
        gt[:, h, :] = GAMMAS[h] ** 128.0
    flags = np.zeros((128, 2), np.float32)
    flags[:, role] = 1.0
    e3 = (p[:, None] == np.arange(3)[None, :]).astype(np.float32)
    parts = [ident, triu, ones, poolA.reshape(128, -1), poolB.reshape(128, -1), poolX.reshape(128, -1),
             poolY.reshape(128, -1), dec.astype(np.float32), vm, pn, flags, e3, gt.reshape(128, -1),
             cos.reshape(128, -1), sin.reshape(128, -1)]
    return (np.ascontiguousarray(np.concatenate(parts, axis=1).astype(np.float32)),
            np.ascontiguousarray(colmask))


def build_program(nch, G=3, pairs=((0, 4), (1, 5), (2, 6), (3, 7)), debug=False):
    depth = 1
    S = (nch - 1) * 128
    NP = nch + G
    nc = bass.Bass("TRN2", target_bir_lowering=False)
    P = Prog(nc)
    din = lambda name, shape: nc.dram_tensor(name, list(shape), F32, kind="ExternalInput").ap()
    xin = din("xin", [NP * 128, D])
    ln_emb_g = din("ln_emb_g", [D]); ln_emb_b = din("ln_emb_b", [D])
    w_in = din("w_in", [depth, D, D_IN]); b_f = din("b_f", [depth, 6])
    ret_gn_g = din("ret_gn_g", [depth, 384]); pool_w = din("pool_w", [depth, 4, 64, 64])
    pool_scale = din("pool_scale", [depth, 256]); w_out = din("w_out", [depth, D, D])
    ln1_g = din("ln1_g", [depth, D]); ln1_b = din("ln1_b", [depth, D])
    w1 = din("w_ffn1", [depth, D, D_FF]); w3 = din("w_ffn3", [depth, D, D_FF]); w2 = din("w_ffn2", [depth, D_FF, D])
    ln2_g = din("ln2_g", [depth, D]); ln2_b = din("ln2_b", [depth, D])
    NCONST = 128 * 3 + 512 * 4 + 8 + 2 * (G + 1) + 2 + 3 + 384 + 48 * NP
    cst_d = din("consts", [128, NCONST])
    cmask_d = din("colmask", [128, (G + 1) * 128])
    out = nc.dram_tensor("out", [S, D], F32, kind="ExternalOutput").ap()
    obuf = nc.dram_tensor("obuf", [G * 128, D], F32, kind="Internal").ap()
    gath = nc.dram_tensor("gath", [2 * G * 128, D], F32, kind="Internal").ap()
    nch_seq = nch
    nch = NP
    wbf = {}
    for l in range(depth):
        wbf[('in', l)] = nc.dram_tensor("wbf_in%d" % l, [D, D_IN], BF16, kind="Internal").ap()
        wbf[('out', l)] = nc.dram_tensor("wbf_out%d" % l, [D, D], BF16, kind="Internal").ap()
        wbf[('w1', l)] = nc.dram_tensor("wbf_w1%d" % l, [D, D_FF], BF16, kind="Internal").ap()
        wbf[('w3', l)] = nc.dram_tensor("wbf_w3%d" % l, [D, D_FF], BF16, kind="Internal").ap()
        wbf[('w2', l)] = nc.dram_tensor("wbf_w2%d" % l, [D_FF, D], BF16, kind="Internal").ap()
    wsrc = {'in': w_in, 'out': w_out, 'w1': w1, 'w3': w3, 'w2': w2}
    RB = 256
    for l in range(depth):
        for nm in ('in', 'out', 'w1', 'w3', 'w2'):
            rows = wbf[(nm, l)].shape[0]
            for r0 in range(0, rows, RB):
                P.dma('pool', wbf[(nm, l)][r0:r0 + RB, :], wsrc[nm][l, r0:r0 + RB, :],
                      writes=['wbf_%s%d_%d' % (nm, l, r0 // RB)])

    def wnames(nm, l, r0, r1):
        return ['wbf_%s%d_%d' % (nm, l, rb) for rb in range(r0 // RB, (r1 - 1) // RB + 1)]

    NT = G * 128
    cst = P.sb("cst", [128, NCONST], F32)
    o = 0
    def cs(n):
        nonlocal o
        v = cst[:, o:o + n]; o += n
        return v
    ident_f = cs(128); triu_f = cs(128); ones_f = cs(128)
    poolA = cs(512).rearrange("p (g t) -> p g t", g=4)
    poolB = cs(512).rearrange("p (g t) -> p g t", g=4)
    poolX = cs(512).rearrange("p (g t) -> p g t", g=4)
    poolY = cs(512).rearrange("p (g t) -> p g t", g=4)
    dec = cs(8); vmaskt = cs(G + 1); padnegt = cs(G + 1); flags = cs(2); e3 = cs(3)
    gtile = cs(384)
    cos = cs(24 * nch).rearrange("p (n f) -> p n f", f=24)
    sin = cs(24 * nch).rearrange("p (n f) -> p n f", f=24)
    P.dma('sp', cst[:], cst_d, writes=['cst'])
    cbf = P.sb("cbf", [128, 384], BF16)
    ident_b = cbf[:, 0:128]; triu_b = cbf[:, 128:256]; ones_b = cbf[:, 256:384]
    P.op('dve', lambda e: e.tensor_copy(cbf[:], cst[:, 0:384]), reads=['cst'], writes=['cbf'])
    cmask_b = P.sb("cmask_b", [128, (G + 1) * 128], BF16)
    _init_later = []
    P.dma('pool', cmask_b[:], cmask_d, writes=['cmask_b'])

    hres = P.sb("hres", [128, G, D], F32)
    hT = P.sb("hT", [128, 8, NT], BF16)
    KT = P.sb("KT", [128, 3, nch * 128], BF16)
    Vst = P.sb("Vst", [128, nch, 384], BF16)
    qT = P.sb("qT", [128, 3, 2, NT], BF16)
    wring = [P.sb("wr%d" % i, [128, SLOT], BF16) for i in range(NSLOT)]
    fall = P.sb("fall", [128, G, 6], F32)
    rt = [P.sb("rt%d" % i, [128, 8, 24], F32) for i in range(4)]
    rr = P.sb("rr", [128, 8, 48], F32)
    qkr = P.sb("qkr", [128, 8, 48], BF16)
    qkT = P.sb("qkT", [48, 8, 128], BF16)
    vr = P.sb("vr", [128, G, 384], BF16)
    gsil = P.sb("gsil", [128, G, 384], BF16)
    ubuf = P.sb("ubuf", [128, 2, 256], F32)
    state_f = P.sb("state_f", [48, 384], F32)
    state_b = P.sb("state_b", [48, 384], BF16)
    sTb = P.sb("sTb", [128, 4, 128], BF16)
    orn = P.sb("orn", [128, 4, 96], F32)
    mixr = P.sb("mixr", [128, 384], BF16)
    st4 = P.sb("st4", [128, 4, 6], F32); mv4 = P.sb("mv4", [128, 4, 2], F32); rs4 = P.sb("rs4", [128, 4], F32)
    pooledT = P.sb("pooledT", [64, 4, 128], F32)
    pwext = P.sb("pwext", [64, 4, 128], F32)
    pscale = P.sb("pscale", [128, 2], F32)
    gng = P.sb("gng", [128, 384], F32)
    bfb = P.sb("bfb", [128, 6], F32)
    ftmp = P.sb("ftmp", [128, 6], F32); logf = P.sb("logf", [128, 6], F32)
    cneg = P.sb("cneg", [128, nch, 6], F32)
    carry = P.sb("carry", [128, nch + 1, 6], F32)
    crep = P.sb("crep", [128, G, 6, 3], F32)
    c3 = [P.sb("c3_%d" % i, [128, NT], BF16) for i in range(2)]
    onesaug = P.sb("onesaug", [128, 128], BF16)
    ptsum = P.sb("ptsum", [128, NT], F32)
    PT = [P.sb("PT%d" % i, [128, NT], BF16) for i in range(3)]
    mixT = P.sb("mixT", [128, 8, NT], BF16)
    gT = P.sb("gT", [128, NFT, NT], BF16)
    stmp = [P.sb("stmp%d" % i, [128, NT], F32) for i in range(2)]
    orn2 = stmp[0][:, 0:384]
    rden = stmp[1]
    xtmp = gT[:, 0:6, :].rearrange("p a b -> p (a b)").bitcast(F32)[:, 0:D]
    hbf = gT[:, 6:9, :].rearrange("p a b -> p (a b)")[:, 0:D]
    qkall = gT[:, 9:15, :].rearrange("p a b -> p (a b)").bitcast(F32).rearrange("p (g c) -> p g c", g=3)
    uall = gT[:, 15:19, :].rearrange("p a b -> p (a b)").bitcast(F32).rearrange("p (g c) -> p g c", g=3)
    GTN = ['gT', 'gTa', 'gTb', 'gTc', 'gTd']
    _ga = gT[:, 0:6, :].rearrange("p a b -> p (a b)")
    c3r = _ga[0:3, 0:2 * NT].bitcast(F32)
    c3s = _ga[0:3, 2 * NT:4 * NT].bitcast(F32)
    c3h = _ga[0:3, 4 * NT:5 * NT]
    c3l = _ga[0:3, 5 * NT:6 * NT]
    lnp = [P.sb("lnp%d" % i, [128, 2, D], F32) for i in range(2)]
    lst = P.sb("lst", [128, 2, 6], F32); lmv = P.sb("lmv", [128, 2], F32); lrs = P.sb("lrs", [128, 1], F32)

    mmb = [P.ps("mm%d" % i, [128, 512], F32) for i in range(2)]
    stb = [P.ps("st%d" % i, [128, 512], F32) for i in range(2)]
    oacc = P.ps("oacc", [128, 512], F32)
    dacc = P.ps("dacc", [128, 512], F32)
    tpb = P.ps("tp", [128, 1024], BF16)
    scb = P.ps("sc", [128, 512], F32)
    rot = {'mm': 0, 'st': 0, 'PT': 0, 'stmp': 0, 'c3': 0, 'lnp': 0}

    def nxt(kind, n):
        i = rot[kind]; rot[kind] = (i + 1) % n
        return i

    def mm(out_, lhsT, rhs, start, stop, reads, writes, inc=True):
        P.op('pe', lambda e: e.matmul(out_, lhsT, rhs, start=start, stop=stop), reads, writes, inc)

    def tp(out_, in_, idn, reads, writes, inc=True):
        P.op('pe', lambda e: e.transpose(out_, in_, idn), reads, writes, inc)

    groups = [list(range(g0, min(g0 + G, nch))) for g0 in range(0, nch, G)]
    plan = []
    for l in range(depth):
        for gi in range(len(groups)):
            for c0, c1 in ((0, 384), (384, 768), (768, 1152), (1152, 1408), (2176, 2560), (2560, 2566), (1408, 1792), (1792, 2176)):
                plan.append(('in', l, c0, c1))
            for qtr in range(4):
                plan.append(('out', l, qtr))
            for fb in range(8):
                plan.append(('w1', l, fb)); plan.append(('w3', l, fb))
            for half in range(2):
                for fb in range(4):
                    plan.append(('w2', l, half, fb))
    ws = {'i': 0, 'issued': 0, 'released': [False] * len(plan)}

    def ws_view(key, slot):
        t = wring[slot]
        if key[0] == 'in':
            c = key[3] - key[2]
            return t[:, 0:8 * c].rearrange("p (k c) -> p k c", k=8)
        if key[0] == 'out':
            return t[:, 0:8 * 256].rearrange("p (k c) -> p k c", k=8)
        if key[0] in ('w1', 'w3'):
            nf = min(3, NFT - 3 * key[2])
            return t[:, 0:8 * nf * 128].rearrange("p (k c) -> p k c", k=8)
        nf = min(6, NFT - 6 * key[3])
        return t[:, 0:nf * 512].rearrange("p (k c) -> p k c", k=nf)

    def ws_issue(j):
        key = plan[j]; slot = j % NSLOT
        v = ws_view(key, slot)
        l = key[1]
        if key[0] == 'in':
            src = wbf[('in', l)][:, key[2]:key[3]].rearrange("(k p) c -> p k c", p=128)
            rd = wnames('in', l, 0, D)
        elif key[0] == 'out':
            src = wbf[('out', l)][:, key[2] * 256:(key[2] + 1) * 256].rearrange("(k p) c -> p k c", p=128)
            rd = wnames('out', l, 0, D)
        elif key[0] in ('w1', 'w3'):
            nf = min(3, NFT - 3 * key[2])
            src = wbf[(key[0], l)][:, key[2] * 384:key[2] * 384 + nf * 128].rearrange("(k p) c -> p k c", p=128)
            rd = wnames(key[0], l, 0, D)
        else:
            nf = min(6, NFT - 6 * key[3])
            r0 = key[3] * 6 * 128
            src = wbf[('w2', l)][r0:r0 + nf * 128, key[2] * 512:(key[2] + 1) * 512].rearrange("(k p) c -> p k c", p=128)
            rd = wnames('w2', l, r0, r0 + nf * 128)
        P.dma('sp', v, src, reads=rd, writes=['wr%d' % slot])

    def ws_pump():
        while ws['issued'] < len(plan) and ws['issued'] < ws['i'] + NSLOT:
            j = ws['issued']
            if j - NSLOT >= 0 and not ws['released'][j - NSLOT]:
                break
            ws_issue(j); ws['issued'] += 1

    def ws_next(kind):
        ws_pump()
        j = ws['i']
        assert plan[j][0] == kind, (plan[j], kind)
        assert ws['issued'] > j
        ws['i'] += 1
        slot = j % NSLOT
        return j, ws_view(plan[j], slot), 'wr%d' % slot

    def ws_done(j):
        ws['released'][j] = True
        ws_pump()

    def layernorm(xap, xname, gap, bap, gbname):
        for c in range(2):
            P.op('dve', lambda e, c=c: e.bn_stats(lst[:, c, :], xap[:, c * 512:(c + 1) * 512]), [xname], ['lst'])
        P.op('dve', lambda e: e.bn_aggr(lmv[:], lst[:]), ['lst'], ['lmv'])
        P.op('act', lambda e: e.activation(lrs[:], lmv[:, 1:2], AF.Ln, bias=LN_EPS), ['lmv'], ['lrs'])
        P.op('act', lambda e: e.activation(lrs[:], lrs[:], AF.Exp, scale=-0.5), ['lrs'], ['lrs'])
        P.op('dve', lambda e: e.tensor_scalar(xap, xap, lmv[:, 0:1], lrs[:, 0:1], ALU.subtract, ALU.mult),
             [xname, 'lmv', 'lrs'], [xname])
        P.op('dve', lambda e: e.tensor_tensor(xap, xap, gap, ALU.mult), [xname, gbname], [xname])
        P.op('pool', lambda e: e.tensor_tensor(xap, xap, bap, ALU.add), [xname, gbname], [xname])

    def load_ln(gvec, bvec):
        i = nxt('lnp', 2)
        nm = 'lnp%d' % i
        P.dma('sp', lnp[i][:, 0, :], gvec.partition_broadcast(128), writes=[nm])
        P.dma('sp', lnp[i][:, 1, :], bvec.partition_broadcast(128), writes=[nm])
        return lnp[i][:, 0, :], lnp[i][:, 1, :], nm

    def to_hT(ci, n, zero_pad):
        hn = 'hres%d' % ci
        P.op('pool', lambda e: e.tensor_copy(hbf[:], hres[:, ci, :]), [hn], ['gTb'])
        tv = tpb[:, :].rearrange("p (k t) -> p k t", k=8)
        for k in range(8):
            tp(tv[:, k, :], hbf[:, k * 128:(k + 1) * 128], ident_b, ['gTb', 'cbf'], ['tp'], inc=(k == 7))
        P.op('act', lambda e: e.copy(hT[:, :, ci * 128:(ci + 1) * 128], tv), ['tp'], ['hT%d' % ci])
        if zero_pad:
            cm = cmask_b[:, n * 128:(n + 1) * 128].unsqueeze(1).broadcast_to([128, 8, 128])
            P.op('pool', lambda e: e.tensor_tensor(hT[:, :, ci * 128:(ci + 1) * 128], hT[:, :, ci * 128:(ci + 1) * 128], cm, ALU.mult),
                 ['hT%d' % ci, 'cmask_b'], ['hT%d' % ci])

    for l in range(depth):
        P.dma('sp', gng[:], ret_gn_g[l].partition_broadcast(128), writes=['gng'])
        P.dma('sp', bfb[:], b_f[l].partition_broadcast(128), writes=['bfb'])
        P.op('dve', lambda e: e.memset(pwext[:], 0.0), [], ['pwext'])
        for g in range(4):
            h0 = (g % 2) * 64
            P.dma('sp', pwext[:, g, h0:h0 + 64], pool_w[l, g], reads=[], writes=['pwext'])
        for r in range(2):
            P.dma('sp', pscale[:, r:r + 1], pool_scale[l, r * 128:(r + 1) * 128].rearrange("(p o) -> p o", o=1), writes=['pscale'])
        P.op('dve', lambda e: e.memset(state_f[:], 0.0), [], ['state_f'])
        P.op('dve', lambda e: e.memset(state_b[:], 0.0), [], ['state_b'])
        P.op('dve', lambda e: e.memset(carry[:, 0, :], 0.0), [], ['carry'])
        P.op('dve', lambda e: e.memset(hres[:, 0, :], 0.0), [], ['hres0'])
        for ci in range(G):
            P.dma('sp', obuf[ci * 128:(ci + 1) * 128, :], hres[:, 0, :], reads=['hres0'], writes=['obuf'])
        fA = flags[:, 0:1]; fB = flags[:, 1:2]
        P.op('pool', lambda e: e.memset(qT[:], 0.0), [], ['qT'])
        for i in range(2):
            P.op('pool', lambda e, i=i: e.memset(c3[i][:], 0.0), [], ['c3_%d' % i])
        P.op('pool', lambda e: e.memset(onesaug[:], 0.0), [], ['onesaug'])
        P.op('pool', lambda e: e.memset(onesaug[0:3, :], 1.0), ['onesaug'], ['onesaug'])

        for grp in groups:
            Gc = len(grp); N = Gc * 128; n0 = grp[0]; n1 = grp[-1] + 1
            hTn = ['hT%d' % ci for ci in range(Gc)]
            P.stage = 'A'
            P.cc("AllGather", [list(pr_) for pr_ in pairs], obuf, gath, reads=['obuf'], writes=['gath'])
            eg, eb, enm = load_ln(ln_emb_g, ln_emb_b)
            for ci, n in enumerate(grp):
                hn = 'hres%d' % ci
                hv = hres[:, ci, :]
                P.dma('sp', hv, gath[ci * 128:(ci + 1) * 128, :], reads=['gath'], writes=[hn])
                P.dma('sp', xtmp, xin[n * 128:(n + 1) * 128, :], writes=['gTa'])
                P.op('dve', lambda e: e.tensor_scalar(hv, hv, fB, None, ALU.mult), [hn, 'cst'], [hn])
                P.op('dve', lambda e: e.scalar_tensor_tensor(hv, xtmp, fA, hv, ALU.mult, ALU.add), [hn, 'gTa', 'cst'], [hn])
                P.op('pool', lambda e: e.tensor_copy(xtmp, hv), [hn], ['gTa'])
                layernorm(hv, hn, eg, eb, enm)
                P.op('dve', lambda e: e.tensor_tensor(hv, hv, xtmp, ALU.subtract), [hn, 'gTa'], [hn])
                P.op('dve', lambda e: e.scalar_tensor_tensor(hv, hv, fA, xtmp, ALU.mult, ALU.add), [hn, 'gTa', 'cst'], [hn])
                to_hT(ci, n, zero_pad=(n <= G))
            P.stage = 'B'
            for bname in ('qk', 'vr', 'gr', 'up', 'vf', 'f'):
                j, wv, wn = ws_next('in')
                ncol = plan[j][3] - plan[j][2]
                for ci, n in enumerate(grp):
                    b = nxt('mm', 2); pm = mmb[b]; pn = 'mm%d' % b
                    for k in range(8):
                        mm(pm[:, 0:ncol], hT[:, k, ci * 128:(ci + 1) * 128], wv[:, k, :], k == 0, k == 7,
                           [hTn[ci], wn], [pn], inc=(k == 7))
                    if bname == 'qk':
                        P.op('act', lambda e, pm=pm, ci=ci: e.copy(qkall[:, ci, :], pm[:, 0:384]), [pn], ['gTc'])
                    elif bname == 'vr':
                        P.op('act', lambda e, pm=pm, ci=ci: e.copy(vr[:, ci, :], pm[:, 0:384]), [pn], ['vr%d' % ci])
                    elif bname == 'gr':
                        P.op('act', lambda e, pm=pm, ci=ci: e.activation(gsil[:, ci, :], pm[:, 0:384], AF.Silu), [pn], ['gsil%d' % ci])
                    elif bname == 'up':
                        P.op('dve', lambda e, pm=pm, ci=ci: e.tensor_copy(uall[:, ci, :], pm[:, 0:256]), [pn], ['gTd'])
                    elif bname == 'vf':
                        P.op('act', lambda e, pm=pm, n=n: e.copy(Vst[:, n, :], pm[:, 0:384]), [pn], ['Vst%d' % n])
                    else:
                        P.op('dve', lambda e, pm=pm, ci=ci: e.tensor_tensor(fall[:, ci, :], pm[:, 0:6], bfb[:], ALU.add),
                             [pn, 'bfb'], ['fall%d' % ci])
                ws_done(j)
            for which in ('qf', 'kf'):
                j, wv, wn = ws_next('in')
                for mt in range(3):
                    b = nxt('mm', 2); pm = mmb[b]; pn = 'mm%d' % b
                    for k in range(8):
                        mm(pm[:, 0:N], wv[:, k, mt * 128:(mt + 1) * 128], hT[:, k, 0:N], k == 0, k == 7,
                           hTn + [wn], [pn], inc=(k == 7))
                    if which == 'qf':
                        P.op('act', lambda e, pm=pm, mt=mt: e.mul(qT[0:64, mt, 0, 0:N], pm[0:64, 0:N], 0.125), [pn], ['qT'])
                        P.op('act', lambda e, pm=pm, mt=mt: e.mul(qT[64:128, mt, 1, 0:N], pm[64:128, 0:N], 0.125), [pn], ['qT'])
                    else:
                        P.op('act', lambda e, pm=pm, mt=mt: e.copy(KT[:, mt, n0 * 128:n0 * 128 + N], pm[:, 0:N]), [pn], ['KT%d' % (n0 // G)])
                ws_done(j)
            KTn = ['KT%d' % gi for gi in range(n0 // G + 1)]

            P.stage = 'C'
            for ci, n in enumerate(grp):
                ccols = slice(ci * 128, (ci + 1) * 128)
                qkv = qkall[:, ci, :].rearrange("p (h d) -> p h d", h=8)
                x1 = qkv[:, :, 0:24]; x2 = qkv[:, :, 24:48]
                cb = cos[:, n, :].unsqueeze(1).broadcast_to([128, 8, 24])
                sb_ = sin[:, n, :].unsqueeze(1).broadcast_to([128, 8, 24])
                qn = 'gTc'
                P.op('dve', lambda e: e.tensor_tensor(rt[0][:], x1, cb, ALU.mult), [qn, 'cst'], ['rt0'])
                P.op('pool', lambda e: e.tensor_tensor(rt[1][:], x2, sb_, ALU.mult), [qn, 'cst'], ['rt1'])
                P.op('dve', lambda e: e.tensor_tensor(rt[2][:], x1, sb_, ALU.mult), [qn, 'cst'], ['rt2'])
                P.op('pool', lambda e: e.tensor_tensor(rt[3][:], x2, cb, ALU.mult), [qn, 'cst'], ['rt3'])
                P.op('dve', lambda e: e.tensor_tensor(rr[:, :, 0:24], rt[0][:], rt[1][:], ALU.subtract), ['rt0', 'rt1'], ['rr'])
                P.op('dve', lambda e: e.tensor_tensor(rr[:, :, 24:48], rt[2][:], rt[3][:], ALU.add), ['rt2', 'rt3'], ['rr'])
                P.op('dve', lambda e: e.tensor_tensor(qkr[:], rr[:], dec.unsqueeze(2).broadcast_to([128, 8, 48]), ALU.mult),
                     ['rr', 'cst'], ['qkr'])
                tq = tpb[0:48, :].rearrange("p (h t) -> p h t", h=8)
                for h in range(8):
                    tp(tq[:, h, :], qkr[:, h, :], ident_b, ['qkr', 'cbf'], ['tp'], inc=(h == 7))
                P.op('act', lambda e: e.copy(qkT[:], tq), ['tp'], ['qkT'])
                scv = scb[:, :].rearrange("p (h t) -> p h t", h=4)
                for h in range(4):
                    mm(scv[:, h, :], qkT[:, 4 + h, :], qkT[:, h, :], True, True, ['qkT'], ['sc'], inc=(h == 3))
                P.op('dve', lambda e: e.tensor_tensor(sTb[:], scv, triu_f.unsqueeze(1).broadcast_to([128, 4, 128]), ALU.mult),
                     ['sc', 'cst'], ['sTb'])
                b = nxt('mm', 2); pm = mmb[b]; pn = 'mm%d' % b
                ov = pm[:, 0:384].rearrange("p (h e) -> p h e", h=4)
                stv = state_b[:, :].rearrange("p (h e) -> p h e", h=4)
                for h in range(4):
                    mm(ov[:, h, :], sTb[:, h, :], vr[:, ci, h * 96:(h + 1) * 96], True, False, ['sTb', 'vr%d' % ci], [pn], inc=False)
                    mm(ov[:, h, :], qkT[:, h, :], stv[:, h, :], False, True, ['qkT', 'state_b'], [pn], inc=(h == 3))
                for h in range(4):
                    P.op('dve', lambda e, h=h: e.bn_stats(st4[:, h, :], ov[:, h, :]), [pn], ['st4'])
                for h in range(4):
                    P.op('dve', lambda e, h=h: e.bn_aggr(mv4[:, h, :], st4[:, h, :]), ['st4'], ['mv4'])
                P.op('act', lambda e: e.activation(rs4[:], mv4[:, :, 1], AF.Ln, bias=LN_EPS), ['mv4'], ['rs4'])
                P.op('act', lambda e: e.activation(rs4[:], rs4[:], AF.Exp, scale=-0.5), ['rs4'], ['rs4'])
                for h in range(4):
                    P.op('dve', lambda e, h=h: e.tensor_scalar(orn[:, h, :], ov[:, h, :], mv4[:, h, 0:1], rs4[:, h:h + 1],
                                                               ALU.subtract, ALU.mult), [pn, 'mv4', 'rs4'], ['orn'])
                P.op('pool', lambda e: e.tensor_tensor(orn2[:], orn[:, :, :].rearrange("p h e -> p (h e)"), gng[:], ALU.mult),
                     ['orn', 'gng'], ['stmp0'])
                P.op('dve', lambda e: e.tensor_tensor(mixr[:], orn2[:], gsil[:, ci, :], ALU.mult), ['stmp0', 'gsil%d' % ci], ['mixr'])
                b = nxt('mm', 2); pk = mmb[b]; pkn = 'mm%d' % b
                for h in range(4):
                    mm(pk[0:48, h * 96:(h + 1) * 96], qkr[:, 4 + h, :], vr[:, ci, h * 96:(h + 1) * 96], True, True,
                       ['qkr', 'vr%d' % ci], [pkn], inc=(h == 3))
                P.op('dve', lambda e: e.tensor_tensor(state_f[:], state_f[:], pk[0:48, 0:384], ALU.add), ['state_f', pkn], ['state_f'])
                P.op('dve', lambda e: e.tensor_tensor(state_f[:], state_f[:], gtile[0:48, :], ALU.mult), ['state_f', 'cst'], ['state_f'])
                P.op('dve', lambda e: e.tensor_copy(state_b[:], state_f[:]), ['state_f'], ['state_b'])
                tm = tpb[:, 0:384].rearrange("p (k t) -> p k t", k=3)
                for k in range(3):
                    tp(tm[:, k, :], mixr[:, k * 128:(k + 1) * 128], ident_b, ['mixr', 'cbf'], ['tp'], inc=(k == 2))
                P.op('act', lambda e: e.copy(mixT[:, 0:3, ccols], tm), ['tp'], ['mixT%d' % ci])

                us = n % 2
                P.op('pool', lambda e: e.tensor_copy(ubuf[:, us, :], uall[:, ci, :]), ['gTd'], ['ubuf%d' % us])
                b = nxt('mm', 2); pp = mmb[b]; ppn = 'mm%d' % b
                ppv = pp[0:64, :].rearrange("p (g t) -> p g t", g=4)
                for g in range(4):
                    if n == 0:
                        mm(ppv[:, g, :], ubuf[:, us, g * 64:(g + 1) * 64], poolX[:, g, :], True, True,
                           ['ubuf%d' % us, 'cst'], [ppn], inc=(g == 3))
                    else:
                        mm(ppv[:, g, :], ubuf[:, us, g * 64:(g + 1) * 64], (poolY if n == G else poolA)[:, g, :], True, False,
                           ['ubuf%d' % us, 'cst'], [ppn], inc=False)
                        mm(ppv[:, g, :], ubuf[:, 1 - us, g * 64:(g + 1) * 64], poolB[:, g, :], False, True,
                           ['ubuf%d' % (1 - us), 'cst'], [ppn], inc=(g == 3))
                P.op('act', lambda e: e.copy(pooledT[:], ppv), [ppn], ['pooledT'])
                b = nxt('mm', 2); py = mmb[b]; pyn = 'mm%d' % b
                pyv = py[:, 0:256].rearrange("p (r t) -> p r t", r=2)
                for g in range(4):
                    mm(pyv[:, g // 2, :], pwext[:, g, :], pooledT[:, g, :], g % 2 == 0, g % 2 == 1,
                       ['pwext', 'pooledT'], [pyn], inc=(g == 3))
                for r in range(2):
                    P.op('dve', lambda e, r=r: e.tensor_scalar(mixT[:, 3 + r, ccols], pyv[:, r, :], pscale[:, r:r + 1], None, ALU.mult),
                         [pyn, 'pscale'], ['mixT%d' % ci])

                P.op('act', lambda e: e.activation(ftmp[:], fall[:, ci, :], AF.Exp, scale=-1.0), ['fall%d' % ci], ['ftmp'])
                P.op('act', lambda e: e.activation(ftmp[:], ftmp[:], AF.Ln, bias=1.0), ['ftmp'], ['ftmp'])
                if n <= G:
                    P.op('dve', lambda e: e.tensor_scalar(logf[:], ftmp[:], -1.0, vmaskt[:, n:n + 1], ALU.mult, ALU.mult), ['ftmp', 'cst'], ['logf'])
                else:
                    P.op('dve', lambda e: e.tensor_scalar(logf[:], ftmp[:], -1.0, None, ALU.mult), ['ftmp'], ['logf'])
                b = nxt('mm', 2); pc = mmb[b]; pcn = 'mm%d' % b
                mm(pc[:, 0:6], triu_f, logf[:], True, True, ['cst', 'logf'], [pcn], inc=False)
                mm(pc[:, 6:12], ones_f, logf[:], True, True, ['cst', 'logf'], [pcn], inc=True)
                P.op('dve', lambda e: e.scalar_tensor_tensor(cneg[:, n, :], pc[:, 0:6], -1.0, carry[:, n, :], ALU.mult, ALU.subtract),
                     [pcn, 'carry'], ['cneg'])
                if n <= G:
                    P.op('dve', lambda e: e.tensor_scalar(cneg[:, n, :], cneg[:, n, :], padnegt[:, n:n + 1], None, ALU.add), ['cneg', 'cst'], ['cneg'])
                P.op('dve', lambda e: e.tensor_tensor(crep[:, ci, :, :], pc[:, 0:6].unsqueeze(2).broadcast_to([128, 6, 3]),
                                                      carry[:, n, :].unsqueeze(2).broadcast_to([128, 6, 3]), ALU.add),
                     [pcn, 'carry'], ['crep%d' % ci])
                P.op('dve', lambda e: e.tensor_tensor(carry[:, n + 1, :], carry[:, n, :], pc[:, 6:12], ALU.add), [pcn, 'carry'], ['carry'])

            P.stage = 'D'
            for h in range(6):
                pr = h // 2; r0 = (h % 2) * 64
                b = nxt('mm', 2); px = mmb[b]; pxn = 'mm%d' % b
                for ci in range(Gc):
                    mm(px[0:3, ci * 128:(ci + 1) * 128], crep[:, ci, h, :], ident_f, True, True, ['crep%d' % ci, 'cst'], [pxn], inc=(ci == Gc - 1))
                k3 = nxt('c3', 2); c3t = c3[k3]; c3n = 'c3_%d' % k3
                P.op('dve', lambda e: e.tensor_copy(c3h[:, 0:N], px[0:3, 0:N]), [pxn], ['gTa'])
                P.op('dve', lambda e: e.tensor_tensor(c3r[:, 0:N], px[0:3, 0:N], c3h[:, 0:N], ALU.subtract), [pxn, 'gTa'], ['gTa'])
                P.op('dve', lambda e: e.tensor_copy(c3l[:, 0:N], c3r[:, 0:N]), ['gTa'], ['gTa'])
                P.op('dve', lambda e: e.tensor_tensor(c3r[:, 0:N], c3r[:, 0:N], c3l[:, 0:N], ALU.subtract), ['gTa', 'gTa'], ['gTa'])
                P.op('dve', lambda e: e.tensor_scalar(c3s[:, 0:N], c3h[:, 0:N], e3[0:3, 0:1], None, ALU.mult), ['gTa', 'cst'], ['gTa'])
                P.op('dve', lambda e: e.scalar_tensor_tensor(c3s[:, 0:N], c3l[:, 0:N], e3[0:3, 1:2], c3s[:, 0:N], ALU.mult, ALU.add), ['gTa', 'gTa', 'cst'], ['gTa'])
                P.op('dve', lambda e: e.scalar_tensor_tensor(c3t[0:3, 0:N], c3r[:, 0:N], e3[0:3, 2:3], c3s[:, 0:N], ALU.mult, ALU.add), ['gTa', 'gTa', 'cst'], [c3n])
                def emit_S(J):
                    ql = max(J, n0) - n0
                    c0 = ql * 128
                    si = nxt('st', 2); sp_ = stb[si]; spn = 'st%d' % si
                    mm(sp_[:, c0:N], KT[:, pr, J * 128:(J + 1) * 128], qT[:, pr, h % 2, c0:N], True, False,
                       [KTn[J // G], 'qT'], [spn], inc=False)
                    mm(sp_[:, c0:N], onesaug[:], c3t[:, c0:N], False, True, ['onesaug', c3n], [spn])
                    return sp_, spn

                def emit_rest(J, sp_, spn):
                    ql = max(J, n0) - n0
                    c0 = ql * 128
                    pi = nxt('PT', 3); pt_ = PT[pi]; ptn = 'PT%d' % pi
                    P.op('act', lambda e, J=J, pt_=pt_, sp_=sp_: e.activation(
                        pt_[:, c0:N], sp_[:, c0:N], AF.Exp, bias=cneg[:, J, h:h + 1], scale=1.0),
                        [spn, 'cneg'], [ptn])
                    if J >= n0:
                        P.op('pool', lambda e, pt_=pt_, c0=c0: e.tensor_tensor(pt_[:, c0:c0 + 128], pt_[:, c0:c0 + 128], triu_b, ALU.mult),
                             [ptn, 'cbf'], [ptn])
                    mm(oacc[:, c0:N], Vst[:, J, pr * 128:(pr + 1) * 128], pt_[:, c0:N], J == 0, J == n1 - 1,
                       ['Vst%d' % J, ptn], ['oacc'], inc=True)
                    if J == 0:
                        P.op('dve', lambda e, pt_=pt_: e.tensor_copy(ptsum[:, 0:N], pt_[:, 0:N]), [ptn], ['ptsum'])
                    else:
                        P.op('dve', lambda e, pt_=pt_, c0=c0: e.tensor_tensor(ptsum[:, c0:N], ptsum[:, c0:N], pt_[:, c0:N], ALU.add),
                             [ptn, 'ptsum'], ['ptsum'])

                cur = emit_S(0)
                for J in range(n1):
                    nxt_s = emit_S(J + 1) if J + 1 < n1 else None
                    emit_rest(J, *cur)
                    cur = nxt_s
                mm(dacc[:, 0:N], ones_f, ptsum[:, 0:N], True, True, ['cst', 'ptsum'], ['dacc'])
                P.op('dve', lambda e: e.tensor_scalar(rden[:, 0:N], dacc[:, 0:N], 1e-30, None, ALU.max), ['dacc'], ['stmp1'])
                P.op('dve', lambda e: e.reciprocal(rden[:, 0:N], rden[:, 0:N]), ['stmp1'], ['stmp1'])
                P.op('dve', lambda e, pr=pr, r0=r0: e.tensor_tensor(mixT[r0:r0 + 64, 5 + pr, 0:N], oacc[r0:r0 + 64, 0:N], rden[r0:r0 + 64, 0:N], ALU.mult),
                     ['oacc', 'stmp1'], ['mixT%d' % ci for ci in range(Gc)])

            P.stage = 'E'
            g1, b1, n1m = load_ln(ln1_g[l], ln1_b[l])
            for qtr in range(4):
                j, wv, wn = ws_next('out')
                for ci, n in enumerate(grp):
                    b = nxt('mm', 2); pm = mmb[b]; pn = 'mm%d' % b
                    for k in range(8):
                        mm(pm[:, 0:256], mixT[:, k, ci * 128:(ci + 1) * 128], wv[:, k, :], k == 0, k == 7,
                           ['mixT%d' % ci, wn], [pn], inc=(k == 7))
                    hv = hres[:, ci, qtr * 256:(qtr + 1) * 256]
                    P.op('dve', lambda e, hv=hv, pm=pm: e.scalar_tensor_tensor(hv, hv, ALPHA, pm[:, 0:256], ALU.mult, ALU.add),
                         ['hres%d' % ci, pn], ['hres%d' % ci])
                ws_done(j)
            for ci, n in enumerate(grp):
                layernorm(hres[:, ci, :], 'hres%d' % ci, g1, b1, n1m)
                to_hT(ci, n, zero_pad=False)

            P.stage = 'F'
            for fb in range(8):
                j1, w1v, w1n = ws_next('w1')
                j3, w3v, w3n = ws_next('w3')
                nf = min(3, NFT - 3 * fb)
                for fi in range(nf):
                    ft = fb * 3 + fi
                    b = nxt('mm', 2); pa = mmb[b]; pan = 'mm%d' % b
                    for k in range(8):
                        mm(pa[:, 0:N], w1v[:, k, fi * 128:(fi + 1) * 128], hT[:, k, 0:N], k == 0, k == 7, hTn + [w1n], [pan], inc=(k == 7))
                    si = nxt('st', 2); pb = stb[si]; pbn = 'st%d' % si
                    for k in range(8):
                        mm(pb[:, 0:N], w3v[:, k, fi * 128:(fi + 1) * 128], hT[:, k, 0:N], k == 0, k == 7, hTn + [w3n], [pbn], inc=(k == 7))
                    ti = nxt('stmp', 2); tt = stmp[ti]; ttn = 'stmp%d' % ti
                    P.op('act', lambda e, pa=pa, tt=tt: e.activation(tt[:, 0:N], pa[:, 0:N], AF.Silu), [pan], [ttn])
                    P.op('dve', lambda e, pb=pb, tt=tt, ft=ft: e.tensor_tensor(gT[:, ft, 0:N], tt[:, 0:N], pb[:, 0:N], ALU.mult), [ttn, pbn], GTN)
                ws_done(j1); ws_done(j3)
            g2, b2, n2m = load_ln(ln2_g[l], ln2_b[l])
            for half in range(2):
                blocks = [ws_next('w2') for _ in range(4)]
                for ci, n in enumerate(grp):
                    b = nxt('mm', 2); pm = mmb[b]; pn = 'mm%d' % b
                    for ft in range(NFT):
                        j, wv, wn = blocks[ft // 6]
                        mm(pm[:, :], gT[:, ft, ci * 128:(ci + 1) * 128], wv[:, ft % 6, :], ft == 0, ft == NFT - 1,
                           GTN + [wn], [pn], inc=(ft == NFT - 1))
                    hv = hres[:, ci, half * 512:(half + 1) * 512]
                    P.op('dve', lambda e, hv=hv, pm=pm: e.scalar_tensor_tensor(hv, hv, ALPHA, pm[:, :], ALU.mult, ALU.add),
                         ['hres%d' % ci, pn], ['hres%d' % ci])
                for j, _, _ in blocks:
                    ws_done(j)
            for ci, n in enumerate(grp):
                layernorm(hres[:, ci, :], 'hres%d' % ci, g2, b2, n2m)
                if n >= G + 1 and n - G - 1 < nch_seq - 1:
                    r_ = (n - G - 1) * 128
                    P.dma('sp', out[r_:r_ + 128, :], hres[:, ci, :], reads=['hres%d' % ci], writes=['out%d' % n])
                P.dma('sp', obuf[ci * 128:(ci + 1) * 128, :], hres[:, ci, :], reads=['hres%d' % ci], writes=['obuf'])
    P.finish(['out'])
    return nc, P


def make_in_maps(inputs, G=3):
    x = np.ascontiguousarray(np.asarray(inputs['x'], dtype=np.float32))
    B, S, _ = x.shape
    nch = S // 128 + 1
    NP = nch + G
    f32 = lambda a: np.ascontiguousarray(np.asarray(a, dtype=np.float32))
    meta = f32(inputs['meta'])
    per_layer = ('w_in', 'b_f', 'ret_gn_g', 'pool_w', 'pool_scale', 'w_out', 'ln1_g', 'ln1_b',
                 'w_ffn1', 'w_ffn3', 'w_ffn2', 'ln2_g', 'ln2_b')
    consts = [host_consts(nch, G, r) for r in range(2)]
    in_maps = []
    for c in range(2 * B):
        role = c // B
        b = c % B
        m = {'ln_emb_g': f32(inputs['ln_emb_g']), 'ln_emb_b': f32(inputs['ln_emb_b'])}
        for k in per_layer:
            m[k] = f32(np.asarray(inputs[k])[role:role + 1])
        xin = np.zeros((NP * 128, D), np.float32)
        if role == 0:
            xin[PADN:128] = meta
            xin[128:128 + S] = x[b]
        m['xin'] = xin
        m['consts'], m['colmask'] = consts[role]
        in_maps.append(m)
    return in_maps, B, nch


def kernel(**inputs):
    G = 3
    in_maps, B, nch = make_in_maps(inputs, G)
    nc, _ = build_program(nch, G, pairs=tuple((b, B + b) for b in range(B)))
    res = run_bass_kernel_spmd(nc, in_maps, core_ids=list(range(2 * B)))
    return np.stack([np.asarray(res.results[B + b]['out'], dtype=np.float32) for b in range(B)], axis=0)
```

```python
import numpy as np
import concourse.bass as bass
import concourse.mybir as mybir
from concourse.bass_utils import run_bass_kernel_spmd

F32 = mybir.dt.float32
BF16 = mybir.dt.bfloat16
AF = mybir.ActivationFunctionType
ALU = mybir.AluOpType


class _Rec:
    def __init__(self):
        self.call = None

    def __getattr__(self, name):
        def f(*a, **k):
            self.call = (name, a, k)
            return self
        return f


def _bind(fn):
    rec = _Rec()
    fn(rec)
    name, a, k = rec.call
    return lambda e: getattr(e, name)(*a, **k)


class Prog:
    ENGS = ('pe', 'act', 'dve', 'pool', 'sp')
    NDMA = 16

    def __init__(self, nc):
        self.nc = nc
        self.ops = {e: [] for e in self.ENGS}
        self.sem = {}
        for e in self.ENGS:
            self.sem['c_' + e] = nc.alloc_semaphore('c_' + e)
        self.dma_sems = {}
        for q in ('sp', 'pool', 'act'):
            self.dma_sems[q] = []
            for i in range(self.NDMA if q == 'sp' else 8):
                nm = 'd%s%d' % (q, i)
                self.sem[nm] = nc.alloc_semaphore(nm)
                self.dma_sems[q].append(nm)
        self.cnt = {s: 0 for s in self.sem}
        self.known = {e: {} for e in self.ENGS}
        self.lastw = {}
        self.readers = {}
        self.pending = {e: [] for e in self.ENGS}
        self.dma_rr = {'sp': 0, 'pool': 0, 'act': 0}
        self.n_ops = 0
        self.stage = ''
        self.tags = {e: [] for e in self.ENGS}

    def sb(self, name, shape, dtype):
        return self.nc.alloc_sbuf_tensor(name, list(shape), dtype)

    def ps(self, name, shape, dtype):
        return self.nc.alloc_psum_tensor(name, list(shape), dtype)

    def _collect(self, reads, writes):
        deps = []
        for r in reads:
            t = self.lastw.get(r)
            if t is not None:
                deps.append(t)
        for w in writes:
            t = self.lastw.get(w)
            if t is not None:
                deps.append(t)
            deps.extend(self.readers.get(w, {}).values())
        return deps

    def _waits(self, eng, deps):
        need = {}
        own = 'c_' + eng
        for sem, val in deps:
            if eng == 'pe' and sem == own:
                continue
            if val is None:
                raise RuntimeError("dependency on unresolved token %s from %s" % (sem, eng))
            if self.known[eng].get(sem, 0) >= val:
                continue
            if need.get(sem, 0) < val:
                need[sem] = val
        for s, v in need.items():
            self.known[eng][s] = v
        return list(need.items())

    def _track(self, tok, reads, writes):
        ws = set(writes)
        for r in reads:
            if r in ws:
                continue
            self.readers.setdefault(r, {})[tok[0]] = tok
        for w in writes:
            self.lastw[w] = tok
            self.readers[w] = {}

    def op(self, eng, fn, reads=(), writes=(), inc=True):
        sem = 'c_' + eng
        tok = [sem, None]
        waits = self._waits(eng, self._collect(reads, writes))
        if inc:
            self.cnt[sem] += 1
            tok[1] = self.cnt[sem]
            for p in self.pending[eng]:
                p[1] = tok[1]
            self.pending[eng] = []
        else:
            self.pending[eng].append(tok)
        self.ops[eng].append((waits, _bind(fn), (sem, 1) if inc else None))
        self.tags[eng].append(self.stage)
        self._track(tok, reads, writes)
        self.n_ops += 1

    def dma(self, eng, out, in_, reads=(), writes=()):
        pool = self.dma_sems[eng]
        i = self.dma_rr[eng]
        self.dma_rr[eng] = (i + 1) % len(pool)
        sem = pool[i]
        deps = self._collect(reads, writes)
        if self.cnt[sem] > 0:
            deps.append([sem, self.cnt[sem]])
        waits = self._waits(eng, deps)
        self.cnt[sem] += 16
        tok = [sem, self.cnt[sem]]
        self.ops[eng].append((waits, lambda e: e.dma_start(out=out, in_=in_), (sem, 16)))
        self.tags[eng].append(self.stage)
        self._track(tok, reads, writes)
        self.n_ops += 1

    def cc(self, kind, groups, in_ap, out_ap, reads=(), writes=()):
        if 'cc' not in self.sem:
            self.sem['cc'] = self.nc.alloc_semaphore('cc')
            self.cnt['cc'] = 0
        waits = self._waits('pool', self._collect(reads, writes))
        self.cnt['cc'] += 1
        tok = ['cc', self.cnt['cc']]
        self.ops['pool'].append((waits, lambda e: e.collective_compute(kind, ALU.bypass, replica_groups=groups,
                                                                        ins=[in_ap], outs=[out_ap]), ('cc', 1)))
        self.tags['pool'].append(self.stage)
        self._track(tok, reads, writes)
        self.n_ops += 1

    def finish(self, out_bufs):
        deps = []
        for b in out_bufs:
            t = self.lastw.get(b)
            if t is not None:
                deps.append(t)
        for pool in self.dma_sems.values():
            for s in pool:
                if self.cnt[s] > 0:
                    deps.append([s, self.cnt[s]])
        if self.cnt.get('cc', 0) > 0:
            deps.append(['cc', self.cnt['cc']])
        waits = self._waits('sp', deps)
        self.ops['sp'].append((waits, None, None))
        self._emit()

    def _replay(self, name, eng):
        for waits, fn, inc in self.ops[name]:
            for s, v in waits:
                eng.wait_ge(self.sem[s], v)
            if fn is None:
                continue
            ins = fn(eng)
            if inc is not None:
                ins.then_inc(self.sem[inc[0]], inc[1])

    def _emit(self):
        with self.nc.Block() as block:
            @block.tensor
            def _(e):
                self._replay('pe', e)

            @block.scalar
            def _(e):
                self._replay('act', e)

            @block.vector
            def _(e):
                self._replay('dve', e)

            @block.gpsimd
            def _(e):
                self._replay('pool', e)

            @block.sync
            def _(e):
                self._replay('sp', e)


D = 1024
D_IN = 2566
D_FF = 2816
NFT = D_FF // 128
PADN = 112
ALPHA = 4.0 ** 0.25
LN_EPS = 1e-5
GAMMAS = [1.0 - 2.0 ** (-5.0 - h) for h in range(4)]
POOL_WINDOWS = (2, 4, 8, 16)
SLOT = 3072
NSLOT = 6


def host_consts(nch, G, role):
    NP = nch + G
    sh = G if role == 1 else 0
    p = np.arange(128)
    ident = np.eye(128, dtype=np.float32)
    triu = (p[:, None] <= p[None, :]).astype(np.float32)
    ones = np.ones((128, 128), np.float32)
    poolA = np.zeros((128, 4, 128), np.float32)
    poolB = np.zeros((128, 4, 128), np.float32)
    poolC = np.zeros((128, 4, 128), np.float32)
    for g, w in enumerate(POOL_WINDOWS):
        for t in range(128):
            for s_ in range(t - w + 1, t + 1):
                if s_ >= 0:
                    poolA[s_, g, t] += 1.0 / w
                else:
                    poolB[128 + s_, g, t] += 1.0 / w
            poolA[t, g, t] -= 1.0
        for tr in range(16):
            lo = max(tr + 1 - w, 0)
            cnt = tr + 1 - lo
            for sr in range(lo, tr + 1):
                poolC[PADN + sr, g, PADN + tr] += 1.0 / cnt
            poolC[PADN + tr, g, PADN + tr] -= 1.0
    poolX = poolC if role == 0 else poolA
    poolY = poolA if role == 0 else poolC
    dec = np.zeros((128, 8), np.float64)
    for h in range(4):
        dec[:, h] = GAMMAS[h] ** (p + 1.0)
        dec[:, 4 + h] = GAMMAS[h] ** (-(p + 1.0)) * 48.0 ** -0.5
    vm = np.ones((128, G + 1), np.float32)
    pn = np.zeros((128, G + 1), np.float32)
    for n in range(G + 1):
        ln_ = n - sh
        if ln_ < 0:
            vm[:, n] = 0.0; pn[:, n] = -1e30
        elif ln_ == 0:
            vm[:, n] = (p >= PADN); pn[:, n] = np.where(p >= PADN, 0.0, -1e30)
    colmask = np.repeat(vm.T.reshape(1, -1), 128, axis=0).astype(np.float32)
    pos = (np.arange(NP * 128, dtype=np.float32) - np.float32(PADN + sh * 128))
    inv_freq = (np.float32(10000.0) ** (-np.arange(24, dtype=np.float32) / np.float32(24))).astype(np.float32)
    ang = (pos[:, None] * inv_freq[None, :]).astype(np.float32)
    cos = np.cos(ang).astype(np.float32).reshape(NP, 128, 24).transpose(1, 0, 2)
    sin = np.sin(ang).astype(np.float32).reshape(NP, 128, 24).transpose(1, 0, 2)
    gt = np.zeros((128, 4, 96), np.float32)
    for h in range(4):
        gt[:, h, :] = GAMMAS[h] ** 128.0
    flags = np.zeros((128, 2), np.float32)
    flags[:, role] = 1.0
    e3 = (p[:, None] == np.arange(3)[None, :]).astype(np.float32)
    parts = [ident, triu, ones, poolA.reshape(128, -1), poolB.reshape(128, -1), poolX.reshape(128, -1),
             poolY.reshape(128, -1), dec.astype(np.float32), vm, pn, flags, e3, gt.reshape(128, -1),
             cos.reshape(128, -1), sin.reshape(128, -1)]
    return (np.ascontiguousarray(np.concatenate(parts, axis=1).astype(np.float32)),
            np.ascontiguousarray(colmask))


def build_program(nch, G=3, pairs=((0, 4), (1, 5), (2, 6), (3, 7)), debug=False):
    depth = 1
    S = (nch - 1) * 128
    NP = nch + G
    nc = bass.Bass("TRN2", target_bir_lowering=False)
    P = Prog(nc)
    din = lambda name, shape: nc.dram_tensor(name, list(shape), F32, kind="ExternalInput").ap()
    xin = din("xin", [NP * 128, D])
    ln_emb_g = din("ln_emb_g", [D]); ln_emb_b = din("ln_emb_b", [D])
    w_in = din("w_in", [depth, D, D_IN]); b_f = din("b_f", [depth, 6])
    ret_gn_g = din("ret_gn_g", [depth, 384]); pool_w = din("pool_w", [depth, 4, 64, 64])
    pool_scale = din("pool_scale", [depth, 256]); w_out = din("w_out", [depth, D, D])
    ln1_g = din("ln1_g", [depth, D]); ln1_b = din("ln1_b", [depth, D])
    w1 = din("w_ffn1", [depth, D, D_FF]); w3 = din("w_ffn3", [depth, D, D_FF]); w2 = din("w_ffn2", [depth, D_FF, D])
    ln2_g = din("ln2_g", [depth, D]); ln2_b = din("ln2_b", [depth, D])
    NCONST = 128 * 3 + 512 * 4 + 8 + 2 * (G + 1) + 2 + 3 + 384 + 48 * NP
    cst_d = din("consts", [128, NCONST])
    cmask_d = din("colmask", [128, (G + 1) * 128])
    out = nc.dram_tensor("out", [S, D], F32, kind="ExternalOutput").ap()
    obufs = [nc.dram_tensor("obuf%d" % i, [128, D], F32, kind="Internal").ap() for i in range(G)]
    gaths = [nc.dram_tensor("gath%d" % i, [2 * 128, D], F32, kind="Internal").ap() for i in range(G)]
    nch_seq = nch
    nch = NP
    wbf = {}
    for l in range(depth):
        wbf[('in', l)] = nc.dram_tensor("wbf_in%d" % l, [D, D_IN], BF16, kind="Internal").ap()
        wbf[('out', l)] = nc.dram_tensor("wbf_out%d" % l, [D, D], BF16, kind="Internal").ap()
        wbf[('w1', l)] = nc.dram_tensor("wbf_w1%d" % l, [D, D_FF], BF16, kind="Internal").ap()
        wbf[('w3', l)] = nc.dram_tensor("wbf_w3%d" % l, [D, D_FF], BF16, kind="Internal").ap()
        wbf[('w2', l)] = nc.dram_tensor("wbf_w2%d" % l, [D_FF, D], BF16, kind="Internal").ap()
    wsrc = {'in': w_in, 'out': w_out, 'w1': w1, 'w3': w3, 'w2': w2}
    RB = 256
    for l in range(depth):
        for nm in ('in', 'out', 'w1', 'w3', 'w2'):
            rows = wbf[(nm, l)].shape[0]
            for r0 in range(0, rows, RB):
                P.dma('pool', wbf[(nm, l)][r0:r0 + RB, :], wsrc[nm][l, r0:r0 + RB, :],
                      writes=['wbf_%s%d_%d' % (nm, l, r0 // RB)])

    def wnames(nm, l, r0, r1):
        return ['wbf_%s%d_%d' % (nm, l, rb) for rb in range(r0 // RB, (r1 - 1) // RB + 1)]

    NT = G * 128
    cst = P.sb("cst", [128, NCONST], F32)
    o = 0
    def cs(n):
        nonlocal o
        v = cst[:, o:o + n]; o += n
        return v
    ident_f = cs(128); triu_f = cs(128); ones_f = cs(128)
    poolA = cs(512).rearrange("p (g t) -> p g t", g=4)
    poolB = cs(512).rearrange("p (g t) -> p g t", g=4)
    poolX = cs(512).rearrange("p (g t) -> p g t", g=4)
    poolY = cs(512).rearrange("p (g t) -> p g t", g=4)
    dec = cs(8); vmaskt = cs(G + 1); padnegt = cs(G + 1); flags = cs(2); e3 = cs(3)
    gtile = cs(384)
    cos = cs(24 * nch).rearrange("p (n f) -> p n f", f=24)
    sin = cs(24 * nch).rearrange("p (n f) -> p n f", f=24)
    P.dma('sp', cst[:], cst_d, writes=['cst'])
    cbf = P.sb("cbf", [128, 384], BF16)
    ident_b = cbf[:, 0:128]; triu_b = cbf[:, 128:256]; ones_b = cbf[:, 256:384]
    P.op('dve', lambda e: e.tensor_copy(cbf[:], cst[:, 0:384]), reads=['cst'], writes=['cbf'])
    cmask_b = P.sb("cmask_b", [128, (G + 1) * 128], BF16)
    _init_later = []
    P.dma('pool', cmask_b[:], cmask_d, writes=['cmask_b'])

    hres = P.sb("hres", [128, G, D], F32)
    hT = P.sb("hT", [128, 8, NT], BF16)
    KT = P.sb("KT", [128, 3, nch * 128], BF16)
    Vst = P.sb("Vst", [128, nch, 384], BF16)
    qT = P.sb("qT", [128, 3, 2, NT], BF16)
    wring = [P.sb("wr%d" % i, [128, SLOT], BF16) for i in range(NSLOT)]
    fall = P.sb("fall", [128, G, 6], F32)
    rt = [P.sb("rt%d" % i, [128, 8, 24], F32) for i in range(4)]
    rr = P.sb("rr", [128, 8, 48], F32)
    qkr = P.sb("qkr", [128, 8, 48], BF16)
    qkT = P.sb("qkT", [48, 8, 128], BF16)
    vr = P.sb("vr", [128, G, 384], BF16)
    gsil = P.sb("gsil", [128, G, 384], BF16)
    ubuf = P.sb("ubuf", [128, 2, 256], F32)
    state_f = P.sb("state_f", [48, 384], F32)
    state_b = P.sb("state_b", [48, 384], BF16)
    sTb = P.sb("sTb", [128, 4, 128], BF16)
    orn = P.sb("orn", [128, 4, 96], F32)
    mixr = P.sb("mixr", [128, 384], BF16)
    st4 = P.sb("st4", [128, 4, 6], F32); mv4 = P.sb("mv4", [128, 4, 2], F32); rs4 = P.sb("rs4", [128, 4], F32)
    pooledT = P.sb("pooledT", [64, 4, 128], F32)
    pwext = P.sb("pwext", [64, 4, 128], F32)
    pscale = P.sb("pscale", [128, 2], F32)
    gng = P.sb("gng", [128, 384], F32)
    bfb = P.sb("bfb", [128, 6], F32)
    ftmp = P.sb("ftmp", [128, 6], F32); logf = P.sb("logf", [128, 6], F32)
    cneg = P.sb("cneg", [128, nch, 6], F32)
    carry = P.sb("carry", [128, nch + 1, 6], F32)
    crep = P.sb("crep", [128, G, 6, 3], F32)
    c3 = [P.sb("c3_%d" % i, [128, NT], BF16) for i in range(2)]
    onesaug = P.sb("onesaug", [128, 128], BF16)
    ptsum = P.sb("ptsum", [128, NT], F32)
    PT = [P.sb("PT%d" % i, [128, NT], BF16) for i in range(4)]
    mixT = P.sb("mixT", [128, 8, NT], BF16)
    gT = P.sb("gT", [128, NFT, NT], BF16)
    stmp = [P.sb("stmp%d" % i, [128, NT], F32) for i in range(2)]
    orn2 = stmp[0][:, 0:384]
    rden = stmp[1]
    xtmp = gT[:, 0:6, :].rearrange("p a b -> p (a b)").bitcast(F32)[:, 0:D]
    hbf = gT[:, 6:9, :].rearrange("p a b -> p (a b)")[:, 0:D]
    qkall = gT[:, 9:15, :].rearrange("p a b -> p (a b)").bitcast(F32).rearrange("p (g c) -> p g c", g=3)
    uall = gT[:, 15:19, :].rearrange("p a b -> p (a b)").bitcast(F32).rearrange("p (g c) -> p g c", g=3)
    GTN = ['gT', 'gTa', 'gTb', 'gTc', 'gTd']
    _ga = gT[:, 0:6, :].rearrange("p a b -> p (a b)")
    c3r = _ga[0:3, 0:2 * NT].bitcast(F32)
    c3s = _ga[0:3, 2 * NT:4 * NT].bitcast(F32)
    c3h = _ga[0:3, 4 * NT:5 * NT]
    c3l = _ga[0:3, 5 * NT:6 * NT]
    lnp = [P.sb("lnp%d" % i, [128, 2, D], F32) for i in range(2)]
    lst = P.sb("lst", [128, 2, 6], F32); lmv = P.sb("lmv", [128, 2], F32); lrs = P.sb("lrs", [128, 1], F32)

    mmb = [P.ps("mm%d" % i, [128, 512], F32) for i in range(2)]
    stb = [P.ps("st%d" % i, [128, 512], F32) for i in range(2)]
    oacc = P.ps("oacc", [128, 512], F32)
    dacc = P.ps("dacc", [128, 512], F32)
    tpb = P.ps("tp", [128, 1024], BF16)
    scb = P.ps("sc", [128, 512], F32)
    sbanks = [(stb[0], 'st0'), (stb[1], 'st1'), (scb, 'sc'), (mmb[1], 'mm1')]
    rot = {'sq': 0, 'mm': 0, 'st': 0, 'PT': 0, 'stmp': 0, 'c3': 0, 'lnp': 0}

    def nxt(kind, n):
        i = rot[kind]; rot[kind] = (i + 1) % n
        return i

    def mm(out_, lhsT, rhs, start, stop, reads, writes, inc=True):
        P.op('pe', lambda e: e.matmul(out_, lhsT, rhs, start=start, stop=stop), reads, writes, inc)

    def tp(out_, in_, idn, reads, writes, inc=True):
        P.op('pe', lambda e: e.transpose(out_, in_, idn), reads, writes, inc)

    groups = [list(range(g0, min(g0 + G, nch))) for g0 in range(0, nch, G)]
    plan = []
    for l in range(depth):
        for gi in range(len(groups)):
            for c0, c1 in ((0, 384), (384, 768), (768, 1152), (1152, 1408), (2176, 2560), (2560, 2566), (1408, 1792), (1792, 2176)):
                plan.append(('in', l, c0, c1))
            for qtr in range(4):
                plan.append(('out', l, qtr))
            for fb in range(8):
                plan.append(('w1', l, fb)); plan.append(('w3', l, fb))
            for half in range(2):
                for fb in range(4):
                    plan.append(('w2', l, half, fb))
    ws = {'i': 0, 'issued': 0, 'released': [False] * len(plan)}

    def ws_view(key, slot):
        t = wring[slot]
        if key[0] == 'in':
            c = key[3] - key[2]
            return t[:, 0:8 * c].rearrange("p (k c) -> p k c", k=8)
        if key[0] == 'out':
            return t[:, 0:8 * 256].rearrange("p (k c) -> p k c", k=8)
        if key[0] in ('w1', 'w3'):
            nf = min(3, NFT - 3 * key[2])
            return t[:, 0:8 * nf * 128].rearrange("p (k c) -> p k c", k=8)
        nf = min(6, NFT - 6 * key[3])
        return t[:, 0:nf * 512].rearrange("p (k c) -> p k c", k=nf)

    def ws_issue(j):
        key = plan[j]; slot = j % NSLOT
        v = ws_view(key, slot)
        l = key[1]
        if key[0] == 'in':
            src = wbf[('in', l)][:, key[2]:key[3]].rearrange("(k p) c -> p k c", p=128)
            rd = wnames('in', l, 0, D)
        elif key[0] == 'out':
            src = wbf[('out', l)][:, key[2] * 256:(key[2] + 1) * 256].rearrange("(k p) c -> p k c", p=128)
            rd = wnames('out', l, 0, D)
        elif key[0] in ('w1', 'w3'):
            nf = min(3, NFT - 3 * key[2])
            src = wbf[(key[0], l)][:, key[2] * 384:key[2] * 384 + nf * 128].rearrange("(k p) c -> p k c", p=128)
            rd = wnames(key[0], l, 0, D)
        else:
            nf = min(6, NFT - 6 * key[3])
            r0 = key[3] * 6 * 128
            src = wbf[('w2', l)][r0:r0 + nf * 128, key[2] * 512:(key[2] + 1) * 512].rearrange("(k p) c -> p k c", p=128)
            rd = wnames('w2', l, r0, r0 + nf * 128)
        P.dma('sp', v, src, reads=rd, writes=['wr%d' % slot])

    def ws_pump():
        while ws['issued'] < len(plan) and ws['issued'] < ws['i'] + NSLOT:
            j = ws['issued']
            if j - NSLOT >= 0 and not ws['released'][j - NSLOT]:
                break
            ws_issue(j); ws['issued'] += 1

    def ws_next(kind):
        ws_pump()
        j = ws['i']
        assert plan[j][0] == kind, (plan[j], kind)
        assert ws['issued'] > j
        ws['i'] += 1
        slot = j % NSLOT
        return j, ws_view(plan[j], slot), 'wr%d' % slot

    def ws_done(j):
        ws['released'][j] = True
        ws_pump()

    def layernorm(xap, xname, gap, bap, gbname):
        for c in range(2):
            P.op('dve', lambda e, c=c: e.bn_stats(lst[:, c, :], xap[:, c * 512:(c + 1) * 512]), [xname], ['lst'])
        P.op('dve', lambda e: e.bn_aggr(lmv[:], lst[:]), ['lst'], ['lmv'])
        P.op('act', lambda e: e.activation(lrs[:], lmv[:, 1:2], AF.Ln, bias=LN_EPS), ['lmv'], ['lrs'])
        P.op('act', lambda e: e.activation(lrs[:], lrs[:], AF.Exp, scale=-0.5), ['lrs'], ['lrs'])
        P.op('dve', lambda e: e.tensor_scalar(xap, xap, lmv[:, 0:1], lrs[:, 0:1], ALU.subtract, ALU.mult),
             [xname, 'lmv', 'lrs'], [xname])
        P.op('dve', lambda e: e.tensor_tensor(xap, xap, gap, ALU.mult), [xname, gbname], [xname])
        P.op('pool', lambda e: e.tensor_tensor(xap, xap, bap, ALU.add), [xname, gbname], [xname])

    def load_ln(gvec, bvec):
        i = nxt('lnp', 2)
        nm = 'lnp%d' % i
        P.dma('sp', lnp[i][:, 0, :], gvec.partition_broadcast(128), writes=[nm])
        P.dma('sp', lnp[i][:, 1, :], bvec.partition_broadcast(128), writes=[nm])
        return lnp[i][:, 0, :], lnp[i][:, 1, :], nm

    def to_hT(ci, n, zero_pad):
        hn = 'hres%d' % ci
        P.op('pool', lambda e: e.tensor_copy(hbf[:], hres[:, ci, :]), [hn], ['gTb'])
        tv = tpb[:, :].rearrange("p (k t) -> p k t", k=8)
        for k in range(8):
            tp(tv[:, k, :], hbf[:, k * 128:(k + 1) * 128], ident_b, ['gTb', 'cbf'], ['tp'], inc=(k == 7))
        P.op('act', lambda e: e.copy(hT[:, :, ci * 128:(ci + 1) * 128], tv), ['tp'], ['hT%d' % ci])
        if zero_pad:
            cm = cmask_b[:, n * 128:(n + 1) * 128].unsqueeze(1).broadcast_to([128, 8, 128])
            P.op('pool', lambda e: e.tensor_tensor(hT[:, :, ci * 128:(ci + 1) * 128], hT[:, :, ci * 128:(ci + 1) * 128], cm, ALU.mult),
                 ['hT%d' % ci, 'cmask_b'], ['hT%d' % ci])

    for l in range(depth):
        P.dma('sp', gng[:], ret_gn_g[l].partition_broadcast(128), writes=['gng'])
        P.dma('sp', bfb[:], b_f[l].partition_broadcast(128), writes=['bfb'])
        P.op('dve', lambda e: e.memset(pwext[:], 0.0), [], ['pwext'])
        for g in range(4):
            h0 = (g % 2) * 64
            P.dma('sp', pwext[:, g, h0:h0 + 64], pool_w[l, g], reads=[], writes=['pwext'])
        for r in range(2):
            P.dma('sp', pscale[:, r:r + 1], pool_scale[l, r * 128:(r + 1) * 128].rearrange("(p o) -> p o", o=1), writes=['pscale'])
        P.op('dve', lambda e: e.memset(state_f[:], 0.0), [], ['state_f'])
        P.op('dve', lambda e: e.memset(state_b[:], 0.0), [], ['state_b'])
        P.op('dve', lambda e: e.memset(carry[:, 0, :], 0.0), [], ['carry'])
        P.op('dve', lambda e: e.memset(hres[:, 0, :], 0.0), [], ['hres0'])
        for ci in range(G):
            P.dma('sp', gaths[ci][0:128, :], hres[:, 0, :], reads=['hres0'], writes=['gath%d' % ci])
        fA = flags[:, 0:1]; fB = flags[:, 1:2]
        P.op('pool', lambda e: e.memset(qT[:], 0.0), [], ['qT'])
        for i in range(2):
            P.op('pool', lambda e, i=i: e.memset(c3[i][:], 0.0), [], ['c3_%d' % i])
        P.op('pool', lambda e: e.memset(onesaug[:], 0.0), [], ['onesaug'])
        P.op('pool', lambda e: e.memset(onesaug[0:3, :], 1.0), ['onesaug'], ['onesaug'])

        for grp in groups:
            Gc = len(grp); N = Gc * 128; n0 = grp[0]; n1 = grp[-1] + 1
            hTn = ['hT%d' % ci for ci in range(Gc)]
            P.stage = 'A'
            eg, eb, enm = load_ln(ln_emb_g, ln_emb_b)
            for ci, n in enumerate(grp):
                hn = 'hres%d' % ci
                hv = hres[:, ci, :]
                P.dma('sp', hv, gaths[ci][0:128, :], reads=['gath%d' % ci], writes=[hn])
                P.dma('sp', xtmp, xin[n * 128:(n + 1) * 128, :], writes=['gTa'])
                P.op('dve', lambda e: e.tensor_scalar(hv, hv, fB, None, ALU.mult), [hn, 'cst'], [hn])
                P.op('dve', lambda e: e.scalar_tensor_tensor(hv, xtmp, fA, hv, ALU.mult, ALU.add), [hn, 'gTa', 'cst'], [hn])
                P.op('pool', lambda e: e.tensor_copy(xtmp, hv), [hn], ['gTa'])
                layernorm(hv, hn, eg, eb, enm)
                P.op('dve', lambda e: e.tensor_tensor(hv, hv, xtmp, ALU.subtract), [hn, 'gTa'], [hn])
                P.op('dve', lambda e: e.scalar_tensor_tensor(hv, hv, fA, xtmp, ALU.mult, ALU.add), [hn, 'gTa', 'cst'], [hn])
                to_hT(ci, n, zero_pad=(n <= G))
            P.stage = 'B'
            for bname in ('qk', 'vr', 'gr', 'up', 'vf', 'f'):
                j, wv, wn = ws_next('in')
                ncol = plan[j][3] - plan[j][2]
                for ci, n in enumerate(grp):
                    b = nxt('mm', 2); pm = mmb[b]; pn = 'mm%d' % b
                    for k in range(8):
                        mm(pm[:, 0:ncol], hT[:, k, ci * 128:(ci + 1) * 128], wv[:, k, :], k == 0, k == 7,
                           [hTn[ci], wn], [pn], inc=(k == 7))
                    if bname == 'qk':
                        P.op('act', lambda e, pm=pm, ci=ci: e.copy(qkall[:, ci, :], pm[:, 0:384]), [pn], ['gTc'])
                    elif bname == 'vr':
                        P.op('act', lambda e, pm=pm, ci=ci: e.copy(vr[:, ci, :], pm[:, 0:384]), [pn], ['vr%d' % ci])
                    elif bname == 'gr':
                        P.op('act', lambda e, pm=pm, ci=ci: e.activation(gsil[:, ci, :], pm[:, 0:384], AF.Silu), [pn], ['gsil%d' % ci])
                    elif bname == 'up':
                        P.op('dve', lambda e, pm=pm, ci=ci: e.tensor_copy(uall[:, ci, :], pm[:, 0:256]), [pn], ['gTd'])
                    elif bname == 'vf':
                        P.op('act', lambda e, pm=pm, n=n: e.copy(Vst[:, n, :], pm[:, 0:384]), [pn], ['Vst%d' % n])
                    else:
                        P.op('dve', lambda e, pm=pm, ci=ci: e.tensor_tensor(fall[:, ci, :], pm[:, 0:6], bfb[:], ALU.add),
                             [pn, 'bfb'], ['fall%d' % ci])
                ws_done(j)
            for which in ('qf', 'kf'):
                j, wv, wn = ws_next('in')
                for mt in range(3):
                    b = nxt('mm', 2); pm = mmb[b]; pn = 'mm%d' % b
                    for k in range(8):
                        mm(pm[:, 0:N], wv[:, k, mt * 128:(mt + 1) * 128], hT[:, k, 0:N], k == 0, k == 7,
                           hTn + [wn], [pn], inc=(k == 7))
                    if which == 'qf':
                        P.op('act', lambda e, pm=pm, mt=mt: e.mul(qT[0:64, mt, 0, 0:N], pm[0:64, 0:N], 0.125), [pn], ['qT'])
                        P.op('act', lambda e, pm=pm, mt=mt: e.mul(qT[64:128, mt, 1, 0:N], pm[64:128, 0:N], 0.125), [pn], ['qT'])
                    else:
                        P.op('act', lambda e, pm=pm, mt=mt: e.copy(KT[:, mt, n0 * 128:n0 * 128 + N], pm[:, 0:N]), [pn], ['KT%d' % (n0 // G)])
                ws_done(j)
            KTn = ['KT%d' % gi for gi in range(n0 // G + 1)]

            P.stage = 'C'
            for ci, n in enumerate(grp):
                ccols = slice(ci * 128, (ci + 1) * 128)
                qkv = qkall[:, ci, :].rearrange("p (h d) -> p h d", h=8)
                x1 = qkv[:, :, 0:24]; x2 = qkv[:, :, 24:48]
                cb = cos[:, n, :].unsqueeze(1).broadcast_to([128, 8, 24])
                sb_ = sin[:, n, :].unsqueeze(1).broadcast_to([128, 8, 24])
                qn = 'gTc'
                P.op('dve', lambda e: e.tensor_tensor(rt[0][:], x1, cb, ALU.mult), [qn, 'cst'], ['rt0'])
                P.op('pool', lambda e: e.tensor_tensor(rt[1][:], x2, sb_, ALU.mult), [qn, 'cst'], ['rt1'])
                P.op('dve', lambda e: e.tensor_tensor(rt[2][:], x1, sb_, ALU.mult), [qn, 'cst'], ['rt2'])
                P.op('pool', lambda e: e.tensor_tensor(rt[3][:], x2, cb, ALU.mult), [qn, 'cst'], ['rt3'])
                P.op('dve', lambda e: e.tensor_tensor(rr[:, :, 0:24], rt[0][:], rt[1][:], ALU.subtract), ['rt0', 'rt1'], ['rr'])
                P.op('dve', lambda e: e.tensor_tensor(rr[:, :, 24:48], rt[2][:], rt[3][:], ALU.add), ['rt2', 'rt3'], ['rr'])
                P.op('dve', lambda e: e.tensor_tensor(qkr[:], rr[:], dec.unsqueeze(2).broadcast_to([128, 8, 48]), ALU.mult),
                     ['rr', 'cst'], ['qkr'])
                tq = tpb[0:48, :].rearrange("p (h t) -> p h t", h=8)
                for h in range(8):
                    tp(tq[:, h, :], qkr[:, h, :], ident_b, ['qkr', 'cbf'], ['tp'], inc=(h == 7))
                P.op('act', lambda e: e.copy(qkT[:], tq), ['tp'], ['qkT'])
                scv = scb[:, :].rearrange("p (h t) -> p h t", h=4)
                for h in range(4):
                    mm(scv[:, h, :], qkT[:, 4 + h, :], qkT[:, h, :], True, True, ['qkT'], ['sc'], inc=(h == 3))
                P.op('dve', lambda e: e.tensor_tensor(sTb[:], scv, triu_f.unsqueeze(1).broadcast_to([128, 4, 128]), ALU.mult),
                     ['sc', 'cst'], ['sTb'])
                b = nxt('mm', 2); pm = mmb[b]; pn = 'mm%d' % b
                ov = pm[:, 0:384].rearrange("p (h e) -> p h e", h=4)
                stv = state_b[:, :].rearrange("p (h e) -> p h e", h=4)
                for h in range(4):
                    mm(ov[:, h, :], sTb[:, h, :], vr[:, ci, h * 96:(h + 1) * 96], True, False, ['sTb', 'vr%d' % ci], [pn], inc=False)
                    mm(ov[:, h, :], qkT[:, h, :], stv[:, h, :], False, True, ['qkT', 'state_b'], [pn], inc=(h == 3))
                for h in range(4):
                    P.op('dve', lambda e, h=h: e.bn_stats(st4[:, h, :], ov[:, h, :]), [pn], ['st4'])
                for h in range(4):
                    P.op('dve', lambda e, h=h: e.bn_aggr(mv4[:, h, :], st4[:, h, :]), ['st4'], ['mv4'])
                P.op('act', lambda e: e.activation(rs4[:], mv4[:, :, 1], AF.Ln, bias=LN_EPS), ['mv4'], ['rs4'])
                P.op('act', lambda e: e.activation(rs4[:], rs4[:], AF.Exp, scale=-0.5), ['rs4'], ['rs4'])
                for h in range(4):
                    P.op('dve', lambda e, h=h: e.tensor_scalar(orn[:, h, :], ov[:, h, :], mv4[:, h, 0:1], rs4[:, h:h + 1],
                                                               ALU.subtract, ALU.mult), [pn, 'mv4', 'rs4'], ['orn'])
                P.op('pool', lambda e: e.tensor_tensor(orn2[:], orn[:, :, :].rearrange("p h e -> p (h e)"), gng[:], ALU.mult),
                     ['orn', 'gng'], ['stmp0'])
                P.op('dve', lambda e: e.tensor_tensor(mixr[:], orn2[:], gsil[:, ci, :], ALU.mult), ['stmp0', 'gsil%d' % ci], ['mixr'])
                b = nxt('mm', 2); pk = mmb[b]; pkn = 'mm%d' % b
                for h in range(4):
                    mm(pk[0:48, h * 96:(h + 1) * 96], qkr[:, 4 + h, :], vr[:, ci, h * 96:(h + 1) * 96], True, True,
                       ['qkr', 'vr%d' % ci], [pkn], inc=(h == 3))
                P.op('dve', lambda e: e.tensor_tensor(state_f[:], state_f[:], pk[0:48, 0:384], ALU.add), ['state_f', pkn], ['state_f'])
                P.op('dve', lambda e: e.tensor_tensor(state_f[:], state_f[:], gtile[0:48, :], ALU.mult), ['state_f', 'cst'], ['state_f'])
                P.op('dve', lambda e: e.tensor_copy(state_b[:], state_f[:]), ['state_f'], ['state_b'])
                tm = tpb[:, 0:384].rearrange("p (k t) -> p k t", k=3)
                for k in range(3):
                    tp(tm[:, k, :], mixr[:, k * 128:(k + 1) * 128], ident_b, ['mixr', 'cbf'], ['tp'], inc=(k == 2))
                P.op('act', lambda e: e.copy(mixT[:, 0:3, ccols], tm), ['tp'], ['mixT%d' % ci])

                us = n % 2
                P.op('pool', lambda e: e.tensor_copy(ubuf[:, us, :], uall[:, ci, :]), ['gTd'], ['ubuf%d' % us])
                b = nxt('mm', 2); pp = mmb[b]; ppn = 'mm%d' % b
                ppv = pp[0:64, :].rearrange("p (g t) -> p g t", g=4)
                for g in range(4):
                    if n == 0:
                        mm(ppv[:, g, :], ubuf[:, us, g * 64:(g + 1) * 64], poolX[:, g, :], True, True,
                           ['ubuf%d' % us, 'cst'], [ppn], inc=(g == 3))
                    else:
                        mm(ppv[:, g, :], ubuf[:, us, g * 64:(g + 1) * 64], (poolY if n == G else poolA)[:, g, :], True, False,
                           ['ubuf%d' % us, 'cst'], [ppn], inc=False)
                        mm(ppv[:, g, :], ubuf[:, 1 - us, g * 64:(g + 1) * 64], poolB[:, g, :], False, True,
                           ['ubuf%d' % (1 - us), 'cst'], [ppn], inc=(g == 3))
                P.op('act', lambda e: e.copy(pooledT[:], ppv), [ppn], ['pooledT'])
                b = nxt('mm', 2); py = mmb[b]; pyn = 'mm%d' % b
                pyv = py[:, 0:256].rearrange("p (r t) -> p r t", r=2)
                for g in range(4):
                    mm(pyv[:, g // 2, :], pwext[:, g, :], pooledT[:, g, :], g % 2 == 0, g % 2 == 1,
                       ['pwext', 'pooledT'], [pyn], inc=(g == 3))
                for r in range(2):
                    P.op('dve', lambda e, r=r: e.tensor_scalar(mixT[:, 3 + r, ccols], pyv[:, r, :], pscale[:, r:r + 1], None, ALU.mult),
                         [pyn, 'pscale'], ['mixT%d' % ci])

                P.op('act', lambda e: e.activation(ftmp[:], fall[:, ci, :], AF.Exp, scale=-1.0), ['fall%d' % ci], ['ftmp'])
                P.op('act', lambda e: e.activation(ftmp[:], ftmp[:], AF.Ln, bias=1.0), ['ftmp'], ['ftmp'])
                if n <= G:
                    P.op('dve', lambda e: e.tensor_scalar(logf[:], ftmp[:], -1.0, vmaskt[:, n:n + 1], ALU.mult, ALU.mult), ['ftmp', 'cst'], ['logf'])
                else:
                    P.op('dve', lambda e: e.tensor_scalar(logf[:], ftmp[:], -1.0, None, ALU.mult), ['ftmp'], ['logf'])
                b = nxt('mm', 2); pc = mmb[b]; pcn = 'mm%d' % b
                mm(pc[:, 0:6], triu_f, logf[:], True, True, ['cst', 'logf'], [pcn], inc=False)
                mm(pc[:, 6:12], ones_f, logf[:], True, True, ['cst', 'logf'], [pcn], inc=True)
                P.op('dve', lambda e: e.scalar_tensor_tensor(cneg[:, n, :], pc[:, 0:6], -1.0, carry[:, n, :], ALU.mult, ALU.subtract),
                     [pcn, 'carry'], ['cneg'])
                if n <= G:
                    P.op('dve', lambda e: e.tensor_scalar(cneg[:, n, :], cneg[:, n, :], padnegt[:, n:n + 1], None, ALU.add), ['cneg', 'cst'], ['cneg'])
                P.op('dve', lambda e: e.tensor_tensor(crep[:, ci, :, :], pc[:, 0:6].unsqueeze(2).broadcast_to([128, 6, 3]),
                                                      carry[:, n, :].unsqueeze(2).broadcast_to([128, 6, 3]), ALU.add),
                     [pcn, 'carry'], ['crep%d' % ci])
                P.op('dve', lambda e: e.tensor_tensor(carry[:, n + 1, :], carry[:, n, :], pc[:, 6:12], ALU.add), [pcn, 'carry'], ['carry'])

            P.stage = 'D'
            for h in range(6):
                pr = h // 2; r0 = (h % 2) * 64
                px = mmb[0]; pxn = 'mm0'
                for ci in range(Gc):
                    mm(px[0:3, ci * 128:(ci + 1) * 128], crep[:, ci, h, :], ident_f, True, True, ['crep%d' % ci, 'cst'], [pxn], inc=(ci == Gc - 1))
                k3 = nxt('c3', 2); c3t = c3[k3]; c3n = 'c3_%d' % k3
                P.op('dve', lambda e: e.tensor_copy(c3h[:, 0:N], px[0:3, 0:N]), [pxn], ['gTa'])
                P.op('dve', lambda e: e.tensor_tensor(c3r[:, 0:N], px[0:3, 0:N], c3h[:, 0:N], ALU.subtract), [pxn, 'gTa'], ['gTa'])
                P.op('dve', lambda e: e.tensor_copy(c3l[:, 0:N], c3r[:, 0:N]), ['gTa'], ['gTa'])
                P.op('dve', lambda e: e.tensor_tensor(c3r[:, 0:N], c3r[:, 0:N], c3l[:, 0:N], ALU.subtract), ['gTa', 'gTa'], ['gTa'])
                P.op('dve', lambda e: e.tensor_scalar(c3s[:, 0:N], c3h[:, 0:N], e3[0:3, 0:1], None, ALU.mult), ['gTa', 'cst'], ['gTa'])
                P.op('dve', lambda e: e.scalar_tensor_tensor(c3s[:, 0:N], c3l[:, 0:N], e3[0:3, 1:2], c3s[:, 0:N], ALU.mult, ALU.add), ['gTa', 'gTa', 'cst'], ['gTa'])
                P.op('dve', lambda e: e.scalar_tensor_tensor(c3t[0:3, 0:N], c3r[:, 0:N], e3[0:3, 2:3], c3s[:, 0:N], ALU.mult, ALU.add), ['gTa', 'gTa', 'cst'], [c3n])
                def emit_S(J):
                    ql = max(J, n0) - n0
                    c0 = ql * 128
                    si = nxt('sq', 4); sp_, spn = sbanks[si]
                    mm(sp_[:, c0:N], KT[:, pr, J * 128:(J + 1) * 128], qT[:, pr, h % 2, c0:N], True, False,
                       [KTn[J // G], 'qT'], [spn], inc=False)
                    mm(sp_[:, c0:N], onesaug[:], c3t[:, c0:N], False, True, ['onesaug', c3n], [spn])
                    return sp_, spn

                def emit_rest(J, sp_, spn):
                    ql = max(J, n0) - n0
                    c0 = ql * 128
                    pi = nxt('PT', 4); pt_ = PT[pi]; ptn = 'PT%d' % pi
                    P.op('act', lambda e, J=J, pt_=pt_, sp_=sp_: e.activation(
                        pt_[:, c0:N], sp_[:, c0:N], AF.Exp, bias=cneg[:, J, h:h + 1], scale=1.0),
                        [spn, 'cneg'], [ptn])
                    if J >= n0:
                        P.op('pool', lambda e, pt_=pt_, c0=c0: e.tensor_tensor(pt_[:, c0:c0 + 128], pt_[:, c0:c0 + 128], triu_b, ALU.mult),
                             [ptn, 'cbf'], [ptn])
                    mm(oacc[:, c0:N], Vst[:, J, pr * 128:(pr + 1) * 128], pt_[:, c0:N], J == 0, J == n1 - 1,
                       ['Vst%d' % J, ptn], ['oacc'], inc=True)
                    if J == 0:
                        P.op('dve', lambda e, pt_=pt_: e.tensor_copy(ptsum[:, 0:N], pt_[:, 0:N]), [ptn], ['ptsum'])
                    else:
                        P.op('dve', lambda e, pt_=pt_, c0=c0: e.tensor_tensor(ptsum[:, c0:N], ptsum[:, c0:N], pt_[:, c0:N], ALU.add),
                             [ptn, 'ptsum'], ['ptsum'])

                DEPTH = 3
                pend = [emit_S(J) for J in range(min(DEPTH, n1))]
                for J in range(n1):
                    cur = pend.pop(0)
                    emit_rest(J, *cur)
                    if J + DEPTH < n1:
                        pend.append(emit_S(J + DEPTH))
                mm(dacc[:, 0:N], ones_f, ptsum[:, 0:N], True, True, ['cst', 'ptsum'], ['dacc'])
                P.op('dve', lambda e: e.tensor_scalar(rden[:, 0:N], dacc[:, 0:N], 1e-30, None, ALU.max), ['dacc'], ['stmp1'])
                P.op('dve', lambda e: e.reciprocal(rden[:, 0:N], rden[:, 0:N]), ['stmp1'], ['stmp1'])
                P.op('dve', lambda e, pr=pr, r0=r0: e.tensor_tensor(mixT[r0:r0 + 64, 5 + pr, 0:N], oacc[r0:r0 + 64, 0:N], rden[r0:r0 + 64, 0:N], ALU.mult),
                     ['oacc', 'stmp1'], ['mixT%d' % ci for ci in range(Gc)])

            P.stage = 'E'
            g1, b1, n1m = load_ln(ln1_g[l], ln1_b[l])
            for qtr in range(4):
                j, wv, wn = ws_next('out')
                for ci, n in enumerate(grp):
                    b = nxt('mm', 2); pm = mmb[b]; pn = 'mm%d' % b
                    for k in range(8):
                        mm(pm[:, 0:256], mixT[:, k, ci * 128:(ci + 1) * 128], wv[:, k, :], k == 0, k == 7,
                           ['mixT%d' % ci, wn], [pn], inc=(k == 7))
                    hv = hres[:, ci, qtr * 256:(qtr + 1) * 256]
                    P.op('dve', lambda e, hv=hv, pm=pm: e.scalar_tensor_tensor(hv, hv, ALPHA, pm[:, 0:256], ALU.mult, ALU.add),
                         ['hres%d' % ci, pn], ['hres%d' % ci])
                ws_done(j)
            for ci, n in enumerate(grp):
                layernorm(hres[:, ci, :], 'hres%d' % ci, g1, b1, n1m)
                to_hT(ci, n, zero_pad=False)

            P.stage = 'F'
            for fb in range(8):
                j1, w1v, w1n = ws_next('w1')
                j3, w3v, w3n = ws_next('w3')
                nf = min(3, NFT - 3 * fb)
                for fi in range(nf):
                    ft = fb * 3 + fi
                    b = nxt('mm', 2); pa = mmb[b]; pan = 'mm%d' % b
                    for k in range(8):
                        mm(pa[:, 0:N], w1v[:, k, fi * 128:(fi + 1) * 128], hT[:, k, 0:N], k == 0, k == 7, hTn + [w1n], [pan], inc=(k == 7))
                    si = nxt('st', 2); pb = stb[si]; pbn = 'st%d' % si
                    for k in range(8):
                        mm(pb[:, 0:N], w3v[:, k, fi * 128:(fi + 1) * 128], hT[:, k, 0:N], k == 0, k == 7, hTn + [w3n], [pbn], inc=(k == 7))
                    ti = nxt('stmp', 2); tt = stmp[ti]; ttn = 'stmp%d' % ti
                    P.op('act', lambda e, pa=pa, tt=tt: e.activation(tt[:, 0:N], pa[:, 0:N], AF.Silu), [pan], [ttn])
                    P.op('dve', lambda e, pb=pb, tt=tt, ft=ft: e.tensor_tensor(gT[:, ft, 0:N], tt[:, 0:N], pb[:, 0:N], ALU.mult), [ttn, pbn], GTN)
                ws_done(j1); ws_done(j3)
            g2, b2, n2m = load_ln(ln2_g[l], ln2_b[l])
            for half in range(2):
                blocks = [ws_next('w2') for _ in range(4)]
                for ci, n in enumerate(grp):
                    b = nxt('mm', 2); pm = mmb[b]; pn = 'mm%d' % b
                    for ft in range(NFT):
                        j, wv, wn = blocks[ft // 6]
                        mm(pm[:, :], gT[:, ft, ci * 128:(ci + 1) * 128], wv[:, ft % 6, :], ft == 0, ft == NFT - 1,
                           GTN + [wn], [pn], inc=(ft == NFT - 1))
                    hv = hres[:, ci, half * 512:(half + 1) * 512]
                    P.op('dve', lambda e, hv=hv, pm=pm: e.scalar_tensor_tensor(hv, hv, ALPHA, pm[:, :], ALU.mult, ALU.add),
                         ['hres%d' % ci, pn], ['hres%d' % ci])
                for j, _, _ in blocks:
                    ws_done(j)
            for ci, n in enumerate(grp):
                layernorm(hres[:, ci, :], 'hres%d' % ci, g2, b2, n2m)
                if n >= G + 1 and n - G - 1 < nch_seq - 1:
                    r_ = (n - G - 1) * 128
                    P.dma('sp', out[r_:r_ + 128, :], hres[:, ci, :], reads=['hres%d' % ci], writes=['out%d' % n])
                if grp is not groups[-1]:
                    P.dma('sp', obufs[ci], hres[:, ci, :], reads=['hres%d' % ci], writes=['obuf%d' % ci])
                    P.cc("AllGather", [list(pr_) for pr_ in pairs], obufs[ci], gaths[ci], reads=['obuf%d' % ci], writes=['gath%d' % ci])
    P.finish(['out'])
    return nc, P


def make_in_maps(inputs, G=3):
    x = np.ascontiguousarray(np.asarray(inputs['x'], dtype=np.float32))
    B, S, _ = x.shape
    nch = S // 128 + 1
    NP = nch + G
    f32 = lambda a: np.ascontiguousarray(np.asarray(a, dtype=np.float32))
    meta = f32(inputs['meta'])
    per_layer = ('w_in', 'b_f', 'ret_gn_g', 'pool_w', 'pool_scale', 'w_out', 'ln1_g', 'ln1_b',
                 'w_ffn1', 'w_ffn3', 'w_ffn2', 'ln2_g', 'ln2_b')
    consts = [host_consts(nch, G, r) for r in range(2)]
    in_maps = []
    for c in range(2 * B):
        role = c // B
        b = c % B
        m = {'ln_emb_g': f32(inputs['ln_emb_g']), 'ln_emb_b': f32(inputs['ln_emb_b'])}
        for k in per_layer:
            m[k] = f32(np.asarray(inputs[k])[role:role + 1])
        xin = np.zeros((NP * 128, D), np.float32)
        if role == 0:
            xin[PADN:128] = meta
            xin[128:128 + S] = x[b]
        m['xin'] = xin
        m['consts'], m['colmask'] = consts[role]
        in_maps.append(m)
    return in_maps, B, nch


def kernel(**inputs):
    G = 3
    in_maps, B, nch = make_in_maps(inputs, G)
    nc, _ = build_program(nch, G, pairs=tuple((b, B + b) for b in range(B)))
    res = run_bass_kernel_spmd(nc, in_maps, core_ids=list(range(2 * B)))
    return np.stack([np.asarray(res.results[B + b]['out'], dtype=np.float32) for b in range(B)], axis=0)
```
